# Optimizing a Trainium2 kernel written in Bass

```python
import math
import jax
import jax.numpy as jnp
from jax import lax
import numpy as np

D_MODEL = 2048
BATCH = 4
SEQ = 4096
DEPTH = 4

ROPE_DIM = 64
ROPE_THETA = 10000.0
Q_BLOCK = 128

DA_HEADS = 8
DA_HEAD_DIM = ROPE_DIM
DA_WIDTH = DA_HEADS * 2 * DA_HEAD_DIM

HY_WIDTH = D_MODEL // 2
HY_ORDER = 2
HY_SHORT = 3
HY_EMB = 33
HY_FILTER_WIDTH = 64
HY_DECAY_TARGET = 1e-2
HY_FAST_DECAY = 0.3
HY_SLOW_DECAY = 1.5

MLA_HEADS = 8
MLA_NOPE = 128
MLA_ROPE = ROPE_DIM
MLA_V = 128
MLA_Q_RANK = D_MODEL // 4
MLA_KV_RANK = D_MODEL // 8
MLA_WIDTH = MLA_HEADS * MLA_V

N_BRANCH = 3
D_FF = 4 * D_MODEL
DEEPNORM_ALPHA = (2 * DEPTH) ** 0.25
DEEPNORM_BETA = (8 * DEPTH) ** -0.25
LN_EPS = 1e-5
RMS_EPS = 1e-6

IN_SPLITS = (DA_WIDTH, DA_WIDTH, DA_WIDTH, 3 * HY_WIDTH, MLA_Q_RANK, MLA_KV_RANK, MLA_ROPE, N_BRANCH * D_MODEL)
D_IN = sum(IN_SPLITS)

kernel_name = 'hybrid_diffattn_hyena_mla_encoder'


def layer_norm(x, g, b):
    xf = x.astype(jnp.float32)
    mu = jnp.mean(xf, -1, keepdims=True)
    var = jnp.mean(jnp.square(xf - mu), -1, keepdims=True)
    return ((xf - mu) * lax.rsqrt(var + LN_EPS) * g + b).astype(x.dtype)


def rms_norm(x, g, eps=RMS_EPS):
    xf = x.astype(jnp.float32)
    y = xf * lax.rsqrt(jnp.mean(xf * xf, -1, keepdims=True) + eps) * g
    return y.astype(x.dtype)


def rope_tables(positions, dim):
    inv = ROPE_THETA ** (-jnp.arange(0, dim, 2, dtype=jnp.float32) / dim)
    ang = positions.astype(jnp.float32)[..., None] * inv
    return jnp.cos(ang), jnp.sin(ang)


def apply_rope(x, cos, sin):
    extra = x.ndim - 3
    shape = cos.shape[:2] + (1,) * extra + cos.shape[-1:]
    c = cos.reshape(shape)
    s = sin.reshape(shape)
    x1, x2 = jnp.split(x.astype(jnp.float32), 2, axis=-1)
    return jnp.concatenate([x1 * c - x2 * s, x2 * c + x1 * s], axis=-1).astype(x.dtype)


def to_blocks(t):
    b, s = t.shape[:2]
    t = t.reshape((b, s // Q_BLOCK, Q_BLOCK) + t.shape[2:])
    return jnp.moveaxis(t, 1, 0)


def from_blocks(t):
    t = jnp.moveaxis(t, 0, 1)
    return t.reshape((t.shape[0], -1) + t.shape[3:])


def diff_attention(u_q, u_k, u_v, cos, sin, lam_vecs, subln_g, lambda_init):
    b, s, _ = u_q.shape
    q = apply_rope(u_q.reshape(b, s, DA_HEADS, 2, DA_HEAD_DIM), cos, sin)
    k = apply_rope(u_k.reshape(b, s, DA_HEADS, 2, DA_HEAD_DIM), cos, sin)
    v = u_v.reshape(b, s, DA_HEADS, 2 * DA_HEAD_DIM)
    lq1, lk1, lq2, lk2 = lam_vecs.astype(jnp.float32)
    lam = jnp.exp(jnp.sum(lq1 * lk1)) - jnp.exp(jnp.sum(lq2 * lk2)) + lambda_init
    scale = DA_HEAD_DIM ** -0.5

    def block(qb):
        sc = jnp.einsum('bqhcd,bkhcd->bhcqk', qb, k).astype(jnp.float32) * scale
        p = jax.nn.softmax(sc, axis=-1)
        w = p[:, :, 0] - lam * p[:, :, 1]
        return jnp.einsum('bhqk,bkhe->bqhe', w.astype(v.dtype), v)

    o = from_blocks(lax.map(block, to_blocks(q)))
    o = rms_norm(o, subln_g, LN_EPS) * (1.0 - lambda_init)
    return o.reshape(b, s, DA_WIDTH)


def mla_attention(c_q, c_kv, k_rope, cos, sin, q_norm_g, kv_norm_g, w_uq, w_ukv):
    b, s, _ = c_q.shape
    q = (rms_norm(c_q, q_norm_g) @ w_uq).reshape(b, s, MLA_HEADS, MLA_NOPE + MLA_ROPE)
    q_nope, q_rope = q[..., :MLA_NOPE], apply_rope(q[..., MLA_NOPE:], cos, sin)
    kv = (rms_norm(c_kv, kv_norm_g) @ w_ukv).reshape(b, s, MLA_HEADS, MLA_NOPE + MLA_V)
    k_nope, v = kv[..., :MLA_NOPE], kv[..., MLA_NOPE:]
    k_r = apply_rope(k_rope, cos, sin)
    scale = (MLA_NOPE + MLA_ROPE) ** -0.5

    def block(qs):
        qn, qr = qs
        sc = jnp.einsum('bqhd,bkhd->bhqk', qn, k_nope) + jnp.einsum('bqhd,bkd->bhqk', qr, k_r)
        p = jax.nn.softmax(sc.astype(jnp.float32) * scale, axis=-1)
        return jnp.einsum('bhqk,bkhd->bqhd', p.astype(v.dtype), v)

    o = from_blocks(lax.map(block, (to_blocks(q_nope), to_blocks(q_rope))))
    return o.reshape(b, s, MLA_WIDTH)


def hyena_features(seq_len):
    t = jnp.linspace(0.0, 1.0, seq_len, dtype=jnp.float32)[:, None]
    bands = (HY_EMB - 1) // 2
    w = 2.0 * math.pi * jnp.arange(seq_len, dtype=jnp.float32)[:, None] / seq_len
    f = jnp.linspace(1e-4, bands - 1, bands, dtype=jnp.float32)[None, :]
    ang = f * w
    z = jnp.concatenate([t, jnp.cos(ang), -jnp.sin(ang)], axis=-1)
    deltas = jnp.linspace(math.log(HY_DECAY_TARGET) / HY_FAST_DECAY,
                          math.log(HY_DECAY_TARGET) / HY_SLOW_DECAY, HY_WIDTH, dtype=jnp.float32)
    decay = jnp.exp(-t * jnp.abs(deltas)[None, :])
    return z, decay


def hyena_filter_spectrum(z, decay, w1, b1, w2, b2, w3, b3, freq, wout):
    seq_len = z.shape[0]
    h = jnp.sin(freq * (z @ w1 + b1))
    h = jnp.sin(freq * (h @ w2 + b2))
    h = jnp.sin(freq * (h @ w3 + b3))
    h = (h @ wout).astype(jnp.float32).reshape(seq_len, HY_ORDER, 2, HY_WIDTH) * decay[:, None, None, :]
    fwd, bwd = h[:, :, 0], h[:, :, 1]
    two_sided = jnp.concatenate([fwd[:1] + bwd[:1], fwd[1:],
                                 jnp.zeros_like(fwd[:1]), bwd[1:][::-1]], axis=0)
    two_sided = two_sided / jnp.sum(jnp.abs(two_sided), axis=0, keepdims=True)
    return jnp.fft.rfft(two_sided, axis=0)


def long_conv(u, k_spec, skip):
    seq_len = u.shape[1]
    u_spec = jnp.fft.rfft(u, n=2 * seq_len, axis=1)
    y = jnp.fft.irfft(u_spec * k_spec, n=2 * seq_len, axis=1)[:, :seq_len]
    return y + u * skip.astype(jnp.float32)


def hyena_mixer(u3, conv_w, conv_b, k_spec, skip):
    c = u3.shape[-1]
    uc = lax.conv_general_dilated(u3, conv_w[:, None, :], window_strides=(1,), padding='SAME',
                                  dimension_numbers=('NWC', 'WIO', 'NWC'),
                                  feature_group_count=c) + conv_b
    v, x1, x2 = jnp.split(uc, 3, axis=-1)
    z = v.astype(jnp.float32)
    for n, gate in enumerate((x1, x2)):
        z = gate.astype(jnp.float32) * long_conv(z, k_spec[:, n], skip[n])
    return z.astype(u3.dtype)


def setup_inputs(seed: int = 0) -> dict:
    key = jax.random.key(seed)
    ks = iter(jax.random.split(key, 40))

    def nrm(shape, scale):
        return jax.random.normal(next(ks), shape, jnp.float32) * scale

    def gain(shape):
        return 1.0 + nrm(shape, 0.02)

    L = DEPTH
    F = HY_FILTER_WIDTH
    return {
        'x': nrm((BATCH, SEQ, D_MODEL), 1.0),
        'positions': jnp.broadcast_to(jnp.arange(SEQ, dtype=jnp.int32), (BATCH, SEQ)),
        'ln_emb_g': gain((D_MODEL,)),
        'ln_emb_b': nrm((D_MODEL,), 0.02),
        'w_in': nrm((L, D_MODEL, D_IN), D_MODEL ** -0.5),
        'gate_b': nrm((L, N_BRANCH * D_MODEL), 0.02),
        'da_lambda': nrm((L, 4, DA_HEAD_DIM), 0.1),
        'da_subln_g': gain((L, 2 * DA_HEAD_DIM)),
        'hy_conv_w': nrm((L, HY_SHORT, 3 * HY_WIDTH), HY_SHORT ** -0.5),
        'hy_conv_b': nrm((L, 3 * HY_WIDTH), 0.02),
        'hy_f_w1': nrm((L, HY_EMB, F), HY_EMB ** -0.5),
        'hy_f_b1': nrm((L, F), 0.02),
        'hy_f_w2': nrm((L, F, F), F ** -0.5),
        'hy_f_b2': nrm((L, F), 0.02),
        'hy_f_w3': nrm((L, F, F), F ** -0.5),
        'hy_f_b3': nrm((L, F), 0.02),
        'hy_f_freq': gain((L, F)),
        'hy_f_wout': nrm((L, F, HY_ORDER * 2 * HY_WIDTH), F ** -0.5),
        'hy_skip': nrm((L, HY_ORDER, HY_WIDTH), 1.0),
        'mla_q_norm_g': gain((L, MLA_Q_RANK)),
        'mla_kv_norm_g': gain((L, MLA_KV_RANK)),
        'mla_w_uq': nrm((L, MLA_Q_RANK, MLA_HEADS * (MLA_NOPE + MLA_ROPE)), MLA_Q_RANK ** -0.5),
        'mla_w_ukv': nrm((L, MLA_KV_RANK, MLA_HEADS * (MLA_NOPE + MLA_V)), MLA_KV_RANK ** -0.5),
        'w_branch_a': nrm((L, DA_WIDTH, D_MODEL), DA_WIDTH ** -0.5),
        'w_branch_b': nrm((L, HY_WIDTH, D_MODEL), HY_WIDTH ** -0.5),
        'w_branch_c': nrm((L, MLA_WIDTH, D_MODEL), MLA_WIDTH ** -0.5),
        'w_out': nrm((L, D_MODEL, D_MODEL), D_MODEL ** -0.5 * DEEPNORM_BETA),
        'ln1_g': gain((L, D_MODEL)),
        'ln1_b': nrm((L, D_MODEL), 0.02),
        'mlp_w1': nrm((L, D_MODEL, D_FF), D_MODEL ** -0.5),
        'mlp_w2': nrm((L, D_FF, D_MODEL), D_FF ** -0.5 * DEEPNORM_BETA),
        'ln2_g': gain((L, D_MODEL)),
        'ln2_b': nrm((L, D_MODEL), 0.02),
    }


def reference(x, positions, ln_emb_g, ln_emb_b, w_in, gate_b, da_lambda, da_subln_g,
              hy_conv_w, hy_conv_b, hy_f_w1, hy_f_b1, hy_f_w2, hy_f_b2, hy_f_w3, hy_f_b3,
              hy_f_freq, hy_f_wout, hy_skip, mla_q_norm_g, mla_kv_norm_g, mla_w_uq, mla_w_ukv,
              w_branch_a, w_branch_b, w_branch_c, w_out, ln1_g, ln1_b, mlp_w1, mlp_w2,
              ln2_g, ln2_b):
    b, s, d = x.shape
    split_points = np.cumsum(IN_SPLITS)[:-1].tolist()
    cos, sin = rope_tables(positions, ROPE_DIM)
    z_pos, decay = hyena_features(s)
    h = layer_norm(x, ln_emb_g, ln_emb_b)
    for l in range(DEPTH):
        lambda_init = 0.8 - 0.6 * math.exp(-0.3 * l)
        proj = h @ w_in[l]
        qa, ka, va, u_hy, c_q, c_kv, k_rope, g = jnp.split(proj, split_points, axis=-1)

        o_a = diff_attention(qa, ka, va, cos, sin, da_lambda[l], da_subln_g[l], lambda_init)
        k_spec = hyena_filter_spectrum(z_pos, decay, hy_f_w1[l], hy_f_b1[l], hy_f_w2[l], hy_f_b2[l],
                                       hy_f_w3[l], hy_f_b3[l], hy_f_freq[l], hy_f_wout[l])
        o_b = hyena_mixer(u_hy, hy_conv_w[l], hy_conv_b[l], k_spec, hy_skip[l])
        o_c = mla_attention(c_q, c_kv, k_rope, cos, sin, mla_q_norm_g[l], mla_kv_norm_g[l],
                            mla_w_uq[l], mla_w_ukv[l])

        gates = jax.nn.sigmoid(g + gate_b[l]).reshape(b, s, N_BRANCH, d)
        merged = (gates[:, :, 0] * (o_a @ w_branch_a[l])
                  + gates[:, :, 1] * (o_b @ w_branch_b[l])
                  + gates[:, :, 2] * (o_c @ w_branch_c[l]))
        h = layer_norm(DEEPNORM_ALPHA * h + merged @ w_out[l], ln1_g[l], ln1_b[l])

        ff = jnp.square(jax.nn.relu(h @ mlp_w1[l])) @ mlp_w2[l]
        h = layer_norm(DEEPNORM_ALPHA * h + ff, ln2_g[l], ln2_b[l])
    return h
```

```python
import math
from contextlib import ExitStack, contextmanager
import numpy as np
import ml_dtypes
import concourse.bass as bass
import concourse.mybir as mybir
from concourse.bass_utils import run_bass_kernel_spmd

F32, BF16, I32 = mybir.dt.float32, mybir.dt.bfloat16, mybir.dt.int32
AF = mybir.ActivationFunctionType
ALU = mybir.AluOpType

S = 4096; D = 2048; DIN = 13120; NL = 4; NT = S // 128
O_QA, O_KA, O_VA, O_U, O_CQ, O_CKV, O_KR, O_G = 0, 1024, 2048, 3072, 6144, 6656, 6912, 6976
ALPHA = float((2 * NL) ** 0.25)
LN_EPS = 1e-5; RMS_EPS = 1e-6
PI = math.pi; TWO_PI = 2.0 * math.pi
NDS_SP = 24; NDS_POOL = 8


class Buf:
    __slots__ = ("w", "r", "excl")

    def __init__(self):
        self.w = None
        self.r = {}
        self.excl = False


class Tl:
    __slots__ = ("t", "b", "lo")

    def __init__(self, t):
        self.t = t
        self.b = Buf()


class Ctx:
    def __init__(self, nc, es):
        self.nc = nc
        self.E = {"pe": nc.tensor, "act": nc.scalar, "dve": nc.vector, "pool": nc.gpsimd, "sp": nc.sync}
        self.semobj = {}
        self.cnt = {}
        for k in self.E:
            self.semobj[k] = es.enter_context(nc.semaphore("s_" + k))
            self.cnt[k] = 0
        self.waited = {k: {} for k in self.E}
        self.dq = {"sp": [("dsp", i) for i in range(NDS_SP)], "pool": [("dpl", i) for i in range(NDS_POOL)]}
        self.dqi = {"sp": 0, "pool": 0}
        for q in self.dq:
            for key in self.dq[q]:
                self.semobj[key] = es.enter_context(nc.semaphore("%s%d" % key))
                self.cnt[key] = 0
        self.nuid = 0
        self.pending = []

    def _flush(self, only_seen=False):
        keep = []
        pend, self.pending = self.pending, []
        for it in pend:
            if only_seen and not it[3]:
                keep.append(it)
            else:
                self._dma_now("sp", it[0], it[1], it[2], ())
        self.pending = keep + self.pending

    def _hazard(self, writes):
        if self.pending and writes:
            ws = set(id(b) for b in writes)
            for it in self.pending:
                for b in it[2]:
                    if id(b) in ws:
                        self._flush()
                        return

    def uid(self, p):
        self.nuid += 1
        return "%s_%d" % (p, self.nuid)

    def _need(self, eng, pts):
        for key, c in pts:
            if c > 0 and self.waited[eng].get(key, 0) < c:
                self.E[eng].wait_ge(self.semobj[key], c)
                self.waited[eng][key] = c

    def _deps(self, eng, reads, writes):
        pts = []
        for b in reads:
            if b.w is not None and not (eng == "pe" and b.w[0] == "pe"):
                pts.append(b.w)
            if b.excl:
                for k, c in b.r.items():
                    if k != eng:
                        pts.append((k, c))
        for b in writes:
            if b.w is not None and b.w[0] != eng:
                pts.append(b.w)
            for k, c in b.r.items():
                if k != eng:
                    pts.append((k, c))
        return pts

    def op(self, eng, fn, reads=(), writes=(), inc=True):
        self._hazard(writes)
        self._need(eng, self._deps(eng, reads, writes))
        ins = fn(self.E[eng])
        if inc:
            self.cnt[eng] += 1
            ins.then_inc(self.semobj[eng], 1)
            c = self.cnt[eng]
        else:
            c = self.cnt[eng] + 1
        for b in reads:
            if b.r.get(eng, 0) < c:
                b.r[eng] = c
        for b in writes:
            b.w = (eng, c)
            b.r = {}

    def dma(self, q, out, in_, reads=(), writes=()):
        if q == "sp" and reads and not writes:
            self._flush(only_seen=True)
            self.pending.append([out, in_, list(reads), False])
            return
        self._hazard(writes)
        self._dma_now(q, out, in_, reads, writes)
        if q == "sp":
            for it in self.pending:
                it[3] = True

    def _dma_now(self, q, out, in_, reads=(), writes=()):
        lst = self.dq[q]
        key = lst[self.dqi[q]]
        self.dqi[q] = (self.dqi[q] + 1) % len(lst)
        pts = self._deps(q, reads, writes)
        pts.append((key, self.cnt[key]))
        self._need(q, pts)
        self.E[q].dma_start(out=out, in_=in_).then_inc(self.semobj[key], 16)
        self.cnt[key] += 16
        c = self.cnt[key]
        for b in reads:
            b.r[key] = c
        for b in writes:
            b.w = (key, c)
            b.r = {}

    def barrier(self):
        self._flush()
        for e in self.E:
            self._need(e, [(k, c) for k, c in self.cnt.items() if k != e])

    @contextmanager
    def phase(self):
        prev = getattr(self, "pes", None)
        with ExitStack() as es:
            self.pes = es
            yield es
            self.barrier()
        self.pes = prev

    def sb(self, shape, dt, name="t"):
        return Tl(self.pes.enter_context(self.nc.sbuf_tensor(self.uid(name), list(shape), dt)))

    def sbn(self, n, shape, dt, name="t"):
        return [self.sb(shape, dt, name) for _ in range(n)]


class Rot:
    def __init__(self, lst):
        self.l = lst
        self.i = 0

    def get(self):
        t = self.l[self.i]
        self.i = (self.i + 1) % len(self.l)
        return t


def build(n_layers=NL, dbg=()):
    nc = bass.Bass("TRN2", target_bir_lowering=False)

    def din(name, shape, dt=F32):
        return nc.dram_tensor(name, list(shape), dt, kind="ExternalInput").ap()

    def dsc(name, shape, dt=F32):
        return nc.dram_tensor(name, list(shape), dt, kind="Internal").ap()

    SHAPES = {
        "x": ([S, D], F32), "pos": ([1, S], I32), "ln_emb_g": ([1, D], F32), "ln_emb_b": ([1, D], F32),
        "w_in": ([NL, D, DIN], F32), "da_lambda": ([NL, 1, 256], F32),
        "hy_conv_w": ([NL, 3, 3072], F32), "hy_conv_b": ([NL, 1, 3072], F32),
        "hy_f_w1": ([NL, 33, 64], F32), "hy_f_w2": ([NL, 64, 64], F32), "hy_f_w3": ([NL, 64, 64], F32),
        "hy_f_wout": ([NL, 64, 4096], F32), "hy_skip": ([NL, 2, 1024], F32),
        "mla_w_uq": ([NL, 512, 1536], F32), "mla_w_ukv": ([NL, 256, 2048], F32),
        "w_branch_a": ([NL, 1024, D], F32), "w_branch_b": ([NL, 1024, D], F32), "w_branch_c": ([NL, 1024, D], F32),
        "w_out": ([NL, D, D], F32), "ln1_g": ([NL, 1, D], F32), "ln1_b": ([NL, 1, D], F32),
        "mlp_w1": ([NL, D, 4 * D], F32), "mlp_w2": ([NL, 4 * D, D], F32), "ln2_g": ([NL, 1, D], F32), "ln2_b": ([NL, 1, D], F32),
        "colp": ([NL, 128, 64], F32), "cst": ([128, 8], F32), "cmat": ([128, 256], BF16), "identf": ([128, 128], F32),
        "wft": ([64, 128, 32, 128], BF16), "wit": ([32, 128, 64, 128], BF16), "zT": ([33, S], F32), "decay": ([S, 1024], F32),
    }

    class LazyIn(dict):
        def __missing__(self, k):
            shp, dt = SHAPES[k]
            v = din(k, shp, dt)
            self[k] = v
            return v
    A = LazyIn()
    nc._used_inputs = A
    OUT = nc.dram_tensor("out", [S, D], F32, kind="ExternalOutput").ap()

    H32 = dsc("H32", [S, D]); HT = dsc("HT", [D, S], BF16)
    QT = dsc("QT", [1024, S], BF16); KT = dsc("KT", [1024, S], BF16); VA = dsc("VA", [S, 1024], BF16)
    USC = dsc("USC", [S + 2, 3072])
    CQG = dsc("CQG", [512, S], BF16); CKVG = dsc("CKVG", [256, S], BF16); KRT = dsc("KRT", [64, S], BF16)
    RSTDQ = dsc("RSTDQ", [128, S]); RSTDKV = dsc("RSTDKV", [128, S])
    GT = dsc("GT", [3 * D, S], BF16)
    QNT = dsc("QNT", [1024, S], BF16); QRT = dsc("QRT", [512, S], BF16); KNT = dsc("KNT", [1024, S], BF16); VC = dsc("VC", [S, 1024], BF16)
    OAT = dsc("OAT", [1024, S], BF16); OBT = dsc("OBT", [1024, S], BF16); OCT = dsc("OCT", [1024, S], BF16)
    VB = dsc("VB", [S, 1024], BF16); VF = dsc("VF", [S, 1024]); X1F = dsc("X1F", [S, 1024]); X2F = dsc("X2F", [S, 1024])
    Z1B = dsc("Z1B", [S, 1024], BF16); Z1F = dsc("Z1F", [S, 1024])
    HSPEC = dsc("HSPEC", [8192, 2048]); SPB = dsc("SPB", [S, 2048], BF16); SMB = dsc("SMB", [S, 2048], BF16)
    MT = dsc("MT", [D, S], BF16)
    ROPEC = dsc("ROPEC", [128, S]); ROPES = dsc("ROPES", [128, S])
    DBG = {}

    with ExitStack() as ges:
        cx = Ctx(nc, ges)
        op = cx.op; dma = cx.dma
        PS = [Tl(ges.enter_context(nc.psum_tensor("ps%d" % i, [128, 512], F32))) for i in range(8)]
        PST = [PS[6], PS[7]]
        for p_ in PS:
            p_.b.excl = True
        cx.pes = ges
        cst = cx.sb([128, 8], F32, "cst")
        cmat = cx.sb([128, 256], BF16, "cmat")
        ones = cx.sb([128, 128], BF16, "ones")
        rstdkvt = cx.sb([128, NT], F32, "rstdkvt")
        dma("sp", cst.t[:], A["cst"][:, :], writes=[cst.b])
        dma("sp", cmat.t[:], A["cmat"][:, :], writes=[cmat.b])
        op("dve", lambda e: e.memset(ones.t[:], 1.0), writes=[ones.b])
        onesf = cx.sb([128, 128], F32, "onesf")
        op("dve", lambda e: e.memset(onesf.t[:], 1.0), writes=[onesf.b])
        identf = cx.sb([128, 128], F32, "identf")
        dma("sp", identf.t[:], A["identf"][:, :], writes=[identf.b])

        def mm(ps, psap, lhsT, rhs, start, stop, reads):
            op("pe", lambda e: e.matmul(psap, lhsT=lhsT, rhs=rhs, start=start, stop=stop), reads=reads, writes=[ps.b], inc=stop)

        PI_LO = 3.1415925

        def rsqrt(tl, ap_out, src_tl, ap_in, scale, eps_col):
            np_ = ap_out.shape[0]
            op("act", lambda e: e.activation(out=ap_out, in_=ap_in, func=AF.Sqrt, bias=cst.t[0:np_, eps_col:eps_col + 1], scale=scale), reads=[src_tl.b, cst.b], writes=[tl.b])
            op("dve", lambda e: e.reciprocal(out=ap_out, in_=ap_out), reads=[tl.b], writes=[tl.b])

        def sin_inplace(tl, ap, ti, tf):
            np_ = ap.shape[0]; nf_ = ap.shape[1]
            tiv = ti.t[0:np_, 0:nf_]; tfv = tf.t[0:np_, 0:nf_]
            op("dve", lambda e: e.tensor_scalar(out=tiv, in0=ap, scalar1=1.0 / TWO_PI, scalar2=None, op0=ALU.mult), reads=[tl.b], writes=[ti.b])
            op("dve", lambda e: e.tensor_copy(out=tfv, in_=tiv), reads=[ti.b], writes=[tf.b])
            op("dve", lambda e: e.scalar_tensor_tensor(out=ap, in0=tfv, scalar=-TWO_PI, in1=ap, op0=ALU.mult, op1=ALU.add), reads=[tf.b, tl.b], writes=[tl.b])
            op("dve", lambda e: e.tensor_scalar(out=tfv, in0=ap, scalar1=PI, scalar2=-TWO_PI, op0=ALU.is_gt, op1=ALU.mult), reads=[tl.b], writes=[tf.b])
            op("dve", lambda e: e.tensor_tensor(out=ap, in0=ap, in1=tfv, op=ALU.add), reads=[tl.b, tf.b], writes=[tl.b])
            op("dve", lambda e: e.tensor_scalar(out=tfv, in0=ap, scalar1=-PI, scalar2=TWO_PI, op0=ALU.is_lt, op1=ALU.mult), reads=[tl.b], writes=[tf.b])
            op("dve", lambda e: e.tensor_tensor(out=ap, in0=ap, in1=tfv, op=ALU.add), reads=[tl.b, tf.b], writes=[tl.b])
            op("dve", lambda e: e.tensor_scalar(out=ap, in0=ap, scalar1=PI_LO, scalar2=-PI_LO, op0=ALU.min, op1=ALU.max), reads=[tl.b], writes=[tl.b])
            op("act", lambda e: e.activation(out=ap, in_=ap, func=AF.Sin), reads=[tl.b], writes=[tl.b])

        LNL = 9
        for d_ in dbg:
            if d_.startswith("lnlvl"):
                LNL = int(d_[5:])

        def ln_tail(xt, Gt, Bt, tok0, dst, tmp):
            stats, mv, rs, yb, hst = tmp["stats"].get(), tmp["mv"].get(), tmp["rs"].get(), tmp["yb"].get(), tmp["hst"].get()
            if LNL >= 2:
                for c in range(4):
                    op("dve", lambda e, c=c: e.bn_stats(out=stats.t[:, c * 6:(c + 1) * 6], in_=xt.t[:, c * 512:(c + 1) * 512]), reads=[xt.b], writes=[stats.b])
                op("dve", lambda e: e.bn_aggr(out=mv.t[:, 0:2], in_=stats.t[:, 0:24]), reads=[stats.b], writes=[mv.b])
            if LNL >= 3:
                rsqrt(rs, rs.t[:, 0:1], mv, mv.t[:, 1:2], 1.0, 3)
                op("dve", lambda e: e.scalar_tensor_tensor(out=rs.t[:, 1:2], in0=mv.t[:, 0:1], scalar=-1.0, in1=rs.t[:, 0:1], op0=ALU.mult, op1=ALU.mult), reads=[mv.b, rs.b], writes=[rs.b])
            if LNL >= 4:
                op("act", lambda e: e.activation(out=xt.t[:], in_=xt.t[:], func=AF.Identity, scale=rs.t[:, 0:1], bias=rs.t[:, 1:2]), reads=[xt.b, rs.b], writes=[xt.b])
                op("dve", lambda e: e.tensor_tensor(out=xt.t[:], in0=xt.t[:], in1=Gt.t[:], op=ALU.mult), reads=[xt.b, Gt.b], writes=[xt.b])
                op("dve", lambda e: e.tensor_tensor(out=xt.t[:], in0=xt.t[:], in1=Bt.t[:], op=ALU.add), reads=[xt.b, Bt.b], writes=[xt.b])
            dma("sp", dst[tok0:tok0 + 128, :], xt.t[:], reads=[xt.b])
            if LNL >= 6:
                for g in range(4):
                    h = PST[g % 2]
                    for j in range(4):
                        kc = g * 4 + j
                        op("pe", lambda e, kc=kc, j=j: e.transpose(h.t[:, j * 128:(j + 1) * 128], xt.t[:, kc * 128:(kc + 1) * 128], identf.t[:]),
                           reads=[xt.b, identf.b], writes=[h.b])
                    if LNL >= 7:
                        if g % 2 == 0:
                            op("dve", lambda e, g=g: e.tensor_copy(out=hst.t[:, g * 4:(g + 1) * 4, :], in_=h.t[:, :].rearrange("p (a b) -> p a b", a=4)), reads=[h.b], writes=[hst.b])
                        else:
                            op("act", lambda e, g=g: e.copy(out=hst.t[:, g * 4:(g + 1) * 4, :], in_=h.t[:, :].rearrange("p (a b) -> p a b", a=4)), reads=[h.b], writes=[hst.b])
            if LNL >= 8:
                dma("sp", HT.rearrange("(kc p) t -> p kc t", p=128)[:, :, tok0:tok0 + 128], hst.t[:], reads=[hst.b])

        def ln_tmp():
            return {"stats": Rot(cx.sbn(2, [128, 24], F32, "st")), "mv": Rot(cx.sbn(2, [128, 2], F32, "mv")),
                    "rs": Rot(cx.sbn(2, [128, 2], F32, "rs")), "yb": Rot([None, None]),
                    "hst": Rot(cx.sbn(2, [128, 16, 128], BF16, "hst"))}

        def load_row_bc(ap_row, n, name):
            t = cx.sb([128, n], F32, name)
            dma("sp", t.t[:], ap_row.partition_broadcast(128), writes=[t.b])
            return t

        with cx.phase():
            posi = cx.sb([128, 1024], I32, "posi"); ang = cx.sb([128, 1024], F32, "ang"); a2 = cx.sb([128, 1024], F32, "a2"); tfr = cx.sb([128, 1024], F32, "tfr")
            zr = cx.sb([1, 3072], F32, "zr")
            op("dve", lambda e: e.memset(zr.t[:], 0.0), writes=[zr.b])
            dma("sp", USC[0:1, :], zr.t[:], reads=[zr.b])
            dma("sp", USC[S + 1:S + 2, :], zr.t[:], reads=[zr.b])
            for c in range(0 if "skiprope" in dbg else 4):
                sl = slice(c * 1024, (c + 1) * 1024)
                dma("sp", posi.t[:], A["pos"][0:1, sl].partition_broadcast(128), writes=[posi.b])
                op("dve", lambda e: e.tensor_copy(out=ang.t[:], in_=posi.t[:]), reads=[posi.b], writes=[ang.b])
                op("dve", lambda e: e.tensor_scalar(out=ang.t[:], in0=ang.t[:], scalar1=cst.t[:, 0:1], scalar2=None, op0=ALU.mult), reads=[ang.b, cst.b], writes=[ang.b])
                op("dve", lambda e: e.tensor_scalar(out=a2.t[:], in0=ang.t[:], scalar1=0.5 * PI, scalar2=None, op0=ALU.add), reads=[ang.b], writes=[a2.b])
                sin_inplace(ang, ang.t[:], posi, tfr)
                op("dve", lambda e: e.tensor_scalar(out=ang.t[:], in0=ang.t[:], scalar1=cst.t[:, 1:2], scalar2=None, op0=ALU.mult), reads=[ang.b, cst.b], writes=[ang.b])
                dma("sp", ROPES[:, sl], ang.t[:], reads=[ang.b])
                sin_inplace(a2, a2.t[:], posi, tfr)
                dma("sp", ROPEC[:, sl], a2.t[:], reads=[a2.b])
            Gt = load_row_bc(A["ln_emb_g"][0:1, :], D, "G"); Bt = load_row_bc(A["ln_emb_b"][0:1, :], D, "B")
            tmp = ln_tmp(); xr = Rot(cx.sbn(2, [128, D], F32, "x"))
            for tt in range(0 if "skipln" in dbg else NT):
                xt = xr.get()
                dma("sp", xt.t[:], A["x"][tt * 128:(tt + 1) * 128, :], writes=[xt.b])
                ln_tail(xt, Gt, Bt, tt * 128, H32, tmp)

        if "p0" in dbg:
            DBG.update(H32=H32, HT=HT, ROPEC=ROPEC, ROPES=ROPES)
        if "p0ln" in dbg:
            DBG.update(H32=H32, HT=HT)

        def rope_epilogue(ps, npart, Ct, St, toks_rel, dst_ap, tp, pre_scale=None):
            xb, t1, t2, ob, ps2 = tp["xb"].get(), tp["t1"].get(), tp["t2"].get(), tp["ob"].get(), tp["ps2"].get()
            P = slice(0, npart)
            src = ps
            if pre_scale is not None:
                xs = tp["xs"].get()
                op("dve", lambda e: e.tensor_tensor(out=xs.t[P, :], in0=ps.t[P, :], in1=pre_scale.t[P, :], op=ALU.mult), reads=[ps.b, pre_scale.b], writes=[xs.b])
                src = xs
            op("act", lambda e: e.copy(out=xb.t[P, :], in_=src.t[P, :]), reads=[src.b], writes=[xb.b])
            mm(ps2, ps2.t[P, :], cmat.t[P, 128:128 + npart], xb.t[P, :], True, True, [cmat.b, xb.b])
            op("dve", lambda e: e.tensor_tensor(out=t1.t[P, :], in0=src.t[P, :], in1=Ct.t[P, toks_rel], op=ALU.mult), reads=[src.b, Ct.b], writes=[t1.b])
            op("dve", lambda e: e.tensor_tensor(out=t2.t[P, :], in0=ps2.t[P, :], in1=St.t[P, toks_rel], op=ALU.mult), reads=[ps2.b, St.b], writes=[t2.b])
            op("dve", lambda e: e.tensor_tensor(out=ob.t[P, :], in0=t1.t[P, :], in1=t2.t[P, :], op=ALU.add), reads=[t1.b, t2.b], writes=[ob.b])
            dma("sp", dst_ap, ob.t[P, :], reads=[ob.b])

        def attn_core(maps, V, qb, psS, psO, accp, Epool, scale):
            res = []
            for mi, terms in enumerate(maps):
                o = psO[mi % len(psO)]
                aD = accp.get(); aP = accp.get()
                sq = {}

                def score(kt):
                    p = psS.get()
                    for ti, (kTl, ksl, qTl, qsl) in enumerate(terms):
                        mm(p, p.t[:, :], kTl.t[ksl, kt * 128:(kt + 1) * 128], qTl.t[qsl, qb * 512:(qb + 1) * 512], ti == 0, ti == len(terms) - 1, [kTl.b, qTl.b])
                    sq[kt] = p
                score(0); score(1)
                for kt in range(NT):
                    p = sq.pop(kt); Et = Epool.get()
                    op("act", lambda e: e.activation(out=Et.t[:], in_=p.t[:], func=AF.Exp, scale=scale), reads=[p.b], writes=[Et.b])
                    if kt + 2 < NT:
                        score(kt + 2)
                    mm(o, o.t[:, :], V.t[:, kt, :], Et.t[:], kt == 0, kt == NT - 1, [V.b, Et.b])
                    eng, a_ = ("pool", aP) if kt % 4 == 3 else ("dve", aD)
                    if kt == 0 or kt == 3:
                        op(eng, lambda e: e.tensor_copy(out=a_.t[:], in_=Et.t[:]), reads=[Et.b], writes=[a_.b])
                    else:
                        op(eng, lambda e: e.tensor_tensor(out=a_.t[:], in0=a_.t[:], in1=Et.t[:], op=ALU.add), reads=[a_.b, Et.b], writes=[a_.b])
                op("dve", lambda e: e.tensor_tensor(out=aD.t[:], in0=aD.t[:], in1=aP.t[:], op=ALU.add), reads=[aD.b, aP.b], writes=[aD.b])
                res.append((o, aD))
            return res

        for l in range(n_layers):
            lam_init = 0.8 - 0.6 * math.exp(-0.3 * l)
            with cx.phase():
                colp = cx.sb([128, 64], F32, "colp")
                dma("sp", colp.t[:], A["colp"][l], writes=[colp.b])
                wr = Rot(cx.sbn(3, [128, 16, 512], BF16, "w"))
                hb = Rot(cx.sbn(1, [128, 16, 1024], BF16, "hTb"))
                Ct = cx.sb([128, 1024], F32, "C"); St = cx.sb([128, 1024], F32, "S")
                tp = {k: Rot(cx.sbn(2, [128, 512], dt, k)) for k, dt in (("xb", BF16), ("t1", F32), ("t2", F32), ("ob", BF16))}
                tp["ps2"] = Rot([PS[5], PS[6]])
                psr = Rot(PS[0:4])
                stg = Rot(cx.sbn(3, [128, 512], BF16, "stg")); stf = Rot(cx.sbn(2, [128, 512], F32, "stf"))
                sqr = Rot(cx.sbn(4, [128, 512], BF16, "sq")); rsb = Rot(cx.sbn(2, [128, 512], F32, "rsb")); rst = Rot(cx.sbn(2, [128, 4], F32, "rst"))
                HTv = HT.rearrange("(kc p) t -> p kc t", p=128)
                Wl = A["w_in"][l]
                P1G = set(["qk", "vu", "cq", "ckv", "g"])
                for d_ in dbg:
                    if d_.startswith("p1:"):
                        P1G = set(d_[3:].split("+"))
                for tb in range(1 if "tb1" in dbg else 4):
                    t0 = tb * 1024
                    hT = hb.get()
                    dma("sp", hT.t[:], HTv[:, :, t0:t0 + 1024], writes=[hT.b])
                    dma("sp", Ct.t[:], ROPEC[:, t0:t0 + 1024], writes=[Ct.b])
                    dma("sp", St.t[:], ROPES[:, t0:t0 + 1024], writes=[St.b])

                    def loadw(c0, ncol):
                        w = wr.get()
                        dma("pool", w.t[:, :, 0:ncol], Wl[:, c0:c0 + ncol].rearrange("(kc p) c -> p kc c", p=128), writes=[w.b])
                        return w

                    def fproj(w, ci, m, ts):
                        p = psr.get()
                        for kc in range(16):
                            mm(p, p.t[0:m, :], w.t[:, kc, ci:ci + m], hT.t[:, kc, ts * 512:(ts + 1) * 512], kc == 0, kc == 15, [w.b, hT.b])
                        return p
                    for (c0, dst) in ((O_QA, QT), (O_QA + 512, QT), (O_KA, KT), (O_KA + 512, KT)) if "qk" in P1G else ():
                        w = loadw(c0, 512)
                        r0 = (c0 - O_QA) % 1024
                        for ts in range(2):
                            for ct in range(4):
                                p = fproj(w, ct * 128, 128, ts)
                                rope_epilogue(p, 128, Ct, St, slice(ts * 512, (ts + 1) * 512), dst[r0 + ct * 128: r0 + (ct + 1) * 128, t0 + ts * 512: t0 + (ts + 1) * 512], tp)
                    for (c0, kind) in ([(O_VA, "va"), (O_VA + 512, "va")] + [(O_U + i * 512, "u") for i in range(6)]) if "vu" in P1G else ():
                        w = loadw(c0, 512)
                        for tt in range(8):
                            p = psr.get()
                            for kc in range(16):
                                mm(p, p.t[:, :], hT.t[:, kc, tt * 128:(tt + 1) * 128], w.t[:, kc, :], kc == 0, kc == 15, [w.b, hT.b])
                            tok = t0 + tt * 128
                            if kind == "va":
                                s_ = stg.get()
                                op("act", lambda e: e.copy(out=s_.t[:], in_=p.t[:]), reads=[p.b], writes=[s_.b])
                                dma("sp", VA[tok:tok + 128, c0 - O_VA:c0 - O_VA + 512], s_.t[:], reads=[s_.b])
                            else:
                                s_ = stf.get()
                                op("dve", lambda e: e.tensor_copy(out=s_.t[:], in_=p.t[:]), reads=[p.b], writes=[s_.b])
                                dma("sp", USC[1 + tok:1 + tok + 128, c0 - O_U:c0 - O_U + 512], s_.t[:], reads=[s_.b])
                    for (c0, nct, gcol, nfeat, dstg, dstr) in ([(O_CQ, 4, 48, 512.0, CQG, RSTDQ)] if "cq" in P1G else []) + ([(O_CKV, 2, 52, 256.0, CKVG, RSTDKV)] if "ckv" in P1G else []):
                        ncol = 512 if nct == 4 else 320
                        w = loadw(c0, ncol)
                        for ts in range(2):
                            tsl = slice(t0 + ts * 512, t0 + (ts + 1) * 512)
                            sqs = []
                            pN = PS[4]
                            for ct in range(nct):
                                p = fproj(w, ct * 128, 128, ts)
                                sq = sqr.get(); s_ = stg.get()
                                op("act", lambda e: e.activation(out=sq.t[:], in_=p.t[:], func=AF.Square), reads=[p.b], writes=[sq.b])
                                op("dve", lambda e: e.tensor_scalar(out=s_.t[:], in0=p.t[:], scalar1=colp.t[:, gcol + ct:gcol + ct + 1], scalar2=None, op0=ALU.mult), reads=[p.b, colp.b], writes=[s_.b])
                                dma("sp", dstg[ct * 128:(ct + 1) * 128, tsl], s_.t[:], reads=[s_.b])
                                mm(pN, pN.t[:, :], ones.t[:], sq.t[:], ct == 0, ct == nct - 1, [ones.b, sq.b])
                                sqs.append(sq)
                            r_ = rsb.get()
                            rsqrt(r_, r_.t[:], pN, pN.t[:], 1.0 / nfeat, 4)
                            dma("sp", dstr[:, tsl], r_.t[:], reads=[r_.b])
                            if nct == 2:
                                pT = PS[4]
                                for t4 in range(4):
                                    for ct in range(2):
                                        mm(pT, pT.t[:, t4:t4 + 1], sqs[ct].t[:, t4 * 128:(t4 + 1) * 128], ones.t[:, 0:1], ct == 0, ct == 1, [sqs[ct].b, ones.b])
                                tile0 = (t0 + ts * 512) // 128
                                r4 = rst.get()
                                rsqrt(rstdkvt, rstdkvt.t[:, tile0:tile0 + 4], pT, pT.t[:, 0:4], 1.0 / nfeat, 4)
                                p = fproj(w, 256, 64, ts)
                                rope_epilogue(p, 64, Ct, St, slice(ts * 512, (ts + 1) * 512), KRT[:, tsl], tp)
                    for gi in range(12 if "g" in P1G else 0):
                        w = loadw(O_G + gi * 512, 512)
                        for ts in range(2):
                            for ct in range(4):
                                p = fproj(w, ct * 128, 128, ts)
                                s_ = stg.get(); j = gi * 4 + ct
                                op("act", lambda e: e.activation(out=s_.t[:], in_=p.t[:], func=AF.Sigmoid, bias=colp.t[:, j:j + 1], scale=1.0), reads=[p.b, colp.b], writes=[s_.b])
                                dma("sp", GT[j * 128:(j + 1) * 128, t0 + ts * 512:t0 + (ts + 1) * 512], s_.t[:], reads=[s_.b])

            if "proj" in dbg and l == 0:
                DBG.update(QT=QT, KT=KT, VA=VA, USC=USC, CQG=CQG, CKVG=CKVG, KRT=KRT, GT=GT, RSTDQ=RSTDQ, HT=HT, H32=H32)
            if "projx" in dbg and l == 0:
                if "qk" in P1G: DBG.update(QT=QT, KT=KT)
                if "vu" in P1G: DBG.update(VA=VA, USC=USC)
                if "cq" in P1G: DBG.update(CQG=CQG, RSTDQ=RSTDQ)
                if "ckv" in P1G: DBG.update(CKVG=CKVG, KRT=KRT, RSTDKV=RSTDKV)
                if "g" in P1G: DBG.update(GT=GT)
            if "stopP1" in dbg:
                break

            with cx.phase():
                colp = cx.sb([128, 64], F32, "colp")
                dma("sp", colp.t[:], A["colp"][l], writes=[colp.b])
                lamt = load_row_bc(A["da_lambda"][l], 256, "lam")
                lw = cx.sb([128, 8], F32, "lw"); lp = cx.sb([128, 128], F32, "lp")
                op("dve", lambda e: e.tensor_tensor(out=lp.t[:, 0:64], in0=lamt.t[:, 0:64], in1=lamt.t[:, 64:128], op=ALU.mult), reads=[lamt.b], writes=[lp.b])
                op("dve", lambda e: e.tensor_tensor(out=lp.t[:, 64:128], in0=lamt.t[:, 128:192], in1=lamt.t[:, 192:256], op=ALU.mult), reads=[lamt.b], writes=[lp.b])
                op("dve", lambda e: e.reduce_sum(out=lw.t[:, 0:2], in_=lp.t[:].rearrange("p (a b) -> p a b", a=2), axis=mybir.AxisListType.X), reads=[lp.b], writes=[lw.b])
                op("act", lambda e: e.activation(out=lw.t[:, 2:4], in_=lw.t[:, 0:2], func=AF.Exp), reads=[lw.b], writes=[lw.b])
                op("dve", lambda e: e.tensor_tensor(out=lw.t[:, 4:5], in0=lw.t[:, 3:4], in1=lw.t[:, 2:3], op=ALU.subtract), reads=[lw.b], writes=[lw.b])
                op("dve", lambda e: e.tensor_scalar(out=lw.t[:, 5:6], in0=lw.t[:, 4:5], scalar1=-lam_init, scalar2=None, op0=ALU.add), reads=[lw.b], writes=[lw.b])
                op("dve", lambda e: e.tensor_scalar(out=lw.t[:, 6:7], in0=colp.t[:, 54:55], scalar1=1.0 - lam_init, scalar2=None, op0=ALU.mult), reads=[colp.b], writes=[lw.b])
                qr_ = Rot(cx.sbn(2, [128, S], BF16, "qT")); kr_ = Rot(cx.sbn(2, [128, S], BF16, "kT")); vr_ = Rot(cx.sbn(2, [128, NT, 128], BF16, "V"))
                Ep = Rot(cx.sbn(4, [128, 512], BF16, "E"))
                psS = Rot(PS[0:4]); psO = [PS[4], PS[5]]; psN = Rot([PS[6], PS[7]])
                accp = Rot(cx.sbn(4, [128, 512], F32, "acc"))
                rec = Rot(cx.sbn(2, [128, 512], F32, "rec")); oc = Rot(cx.sbn(4, [128, 512], F32, "oc"))
                sqb = Rot(cx.sbn(2, [128, 512], BF16, "sqb")); obf = Rot(cx.sbn(2, [128, 512], BF16, "obf"))
                for h in range(8):
                    qT = qr_.get(); kT = kr_.get(); V = vr_.get()
                    dma("sp", qT.t[:], QT[h * 128:(h + 1) * 128, :], writes=[qT.b])
                    dma("sp", kT.t[:], KT[h * 128:(h + 1) * 128, :], writes=[kT.b])
                    dma("sp", V.t[:], VA[:, h * 128:(h + 1) * 128].rearrange("(kt p) e -> p kt e", p=128), writes=[V.b])
                    for qb in range(8):
                        maps = [[(kT, slice(c * 64, (c + 1) * 64), qT, slice(c * 64, (c + 1) * 64))] for c in range(2)]
                        res = attn_core(maps, V, qb, psS, psO, accp, Ep, 0.125)
                        ocs = []
                        for (o, acc_) in res:
                            r_ = rec.get(); oo = oc.get(); n = psN.get()
                            mm(n, n.t[:, :], onesf.t[:], acc_.t[:], True, True, [onesf.b, acc_.b])
                            op("dve", lambda e: e.reciprocal(out=r_.t[:], in_=n.t[:]), reads=[n.b], writes=[r_.b])
                            op("dve", lambda e: e.tensor_tensor(out=oo.t[:], in0=o.t[:], in1=r_.t[:], op=ALU.mult), reads=[o.b, r_.b], writes=[oo.b])
                            ocs.append(oo)
                        od = oc.get()
                        op("dve", lambda e: e.scalar_tensor_tensor(out=od.t[:], in0=ocs[1].t[:], scalar=lw.t[:, 5:6], in1=ocs[0].t[:], op0=ALU.mult, op1=ALU.add), reads=[ocs[0].b, ocs[1].b, lw.b], writes=[od.b])
                        sq = sqb.get(); pR = psS.get(); r_ = rec.get(); ob_ = obf.get()
                        op("act", lambda e: e.activation(out=sq.t[:], in_=od.t[:], func=AF.Square), reads=[od.b], writes=[sq.b])
                        mm(pR, pR.t[:, :], ones.t[:], sq.t[:], True, True, [ones.b, sq.b])
                        rsqrt(r_, r_.t[:], pR, pR.t[:], 1.0 / 128.0, 3)
                        op("dve", lambda e: e.scalar_tensor_tensor(out=ob_.t[:], in0=od.t[:], scalar=lw.t[:, 6:7], in1=r_.t[:], op0=ALU.mult, op1=ALU.mult), reads=[od.b, lw.b, r_.b], writes=[ob_.b])
                        dma("sp", OAT[h * 128:(h + 1) * 128, qb * 512:(qb + 1) * 512], ob_.t[:], reads=[ob_.b])
            if "oa" in dbg and l == 0:
                DBG.update(OAT=OAT)
            if "stopP2" in dbg:
                break

            with cx.phase():
                wqn = cx.sb([128, 4, 8, 128], BF16, "wqn"); wqr = cx.sb([128, 4, 8, 64], BF16, "wqr"); wkv = cx.sb([128, 2, 8, 256], BF16, "wkv")
                wq_v = A["mla_w_uq"][l].rearrange("(kc p) (h x) -> p kc h x", p=128, x=192)
                for kc in range(4):
                    dma("pool", wqn.t[:, kc], wq_v[:, kc, :, 0:128], writes=[wqn.b])
                    dma("pool", wqr.t[:, kc], wq_v[:, kc, :, 128:192], writes=[wqr.b])
                wkv_v = A["mla_w_ukv"][l].rearrange("(kc p) (h x) -> p kc h x", p=128, x=256)
                for kc in range(2):
                    dma("pool", wkv.t[:, kc], wkv_v[:, kc], writes=[wkv.b])
                cq = Rot(cx.sbn(2, [128, 4, 512], BF16, "cq")); ckv = Rot(cx.sbn(2, [128, 2, 512], BF16, "ckv"))
                rq = Rot(cx.sbn(2, [128, 512], F32, "rq")); rk = Rot(cx.sbn(2, [128, 512], F32, "rk"))
                Cr = Rot(cx.sbn(2, [128, 512], F32, "C")); Sr = Rot(cx.sbn(2, [128, 512], F32, "S"))
                tp = {k: Rot(cx.sbn(2, [128, 512], dt, k)) for k, dt in (("xb", BF16), ("t1", F32), ("t2", F32), ("ob", BF16), ("xs", F32))}
                tp["ps2"] = Rot([PS[5], PS[6]])
                psr = Rot(PS[0:5])
                stg = Rot(cx.sbn(3, [128, 512], BF16, "stg"))
                for ts in range(8):
                    tsl = slice(ts * 512, (ts + 1) * 512)
                    cqt = cq.get(); ckt = ckv.get(); rqt = rq.get(); rkt = rk.get(); Ct = Cr.get(); St = Sr.get()
                    dma("sp", cqt.t[:], CQG.rearrange("(kc p) t -> p kc t", p=128)[:, :, tsl], writes=[cqt.b])
                    dma("sp", ckt.t[:], CKVG.rearrange("(kc p) t -> p kc t", p=128)[:, :, tsl], writes=[ckt.b])
                    dma("sp", rqt.t[:], RSTDQ[:, tsl], writes=[rqt.b]); dma("sp", rkt.t[:], RSTDKV[:, tsl], writes=[rkt.b])
                    dma("sp", Ct.t[:], ROPEC[:, tsl], writes=[Ct.b]); dma("sp", St.t[:], ROPES[:, tsl], writes=[St.b])
                    for h in range(8):
                        p = psr.get()
                        for kc in range(4):
                            mm(p, p.t[:, :], wqn.t[:, kc, h, :], cqt.t[:, kc, :], kc == 0, kc == 3, [wqn.b, cqt.b])
                        s_ = stg.get()
                        op("dve", lambda e: e.tensor_tensor(out=s_.t[:], in0=p.t[:], in1=rqt.t[:], op=ALU.mult), reads=[p.b, rqt.b], writes=[s_.b])
                        dma("sp", QNT[h * 128:(h + 1) * 128, tsl], s_.t[:], reads=[s_.b])
                        p = psr.get()
                        for kc in range(2):
                            mm(p, p.t[:, :], wkv.t[:, kc, h, 0:128], ckt.t[:, kc, :], kc == 0, kc == 1, [wkv.b, ckt.b])
                        s_ = stg.get()
                        op("dve", lambda e: e.tensor_tensor(out=s_.t[:], in0=p.t[:], in1=rkt.t[:], op=ALU.mult), reads=[p.b, rkt.b], writes=[s_.b])
                        dma("sp", KNT[h * 128:(h + 1) * 128, tsl], s_.t[:], reads=[s_.b])
                    for hp in range(4):
                        p = psr.get()
                        for kc in range(4):
                            mm(p, p.t[:, :], wqr.t[:, kc, 2 * hp:2 * hp + 2, :].rearrange("p a b -> p (a b)"), cqt.t[:, kc, :], kc == 0, kc == 3, [wqr.b, cqt.b])
                        rope_epilogue(p, 128, Ct, St, slice(0, 512), QRT[hp * 128:(hp + 1) * 128, tsl], tp, pre_scale=rqt)
                    for t4 in range(4):
                        tile = ts * 4 + t4
                        for half in range(2):
                            p = psr.get()
                            for kc in range(2):
                                mm(p, p.t[:, :].rearrange("p (h e) -> p h e", h=4), ckt.t[:, kc, t4 * 128:(t4 + 1) * 128], wkv.t[:, kc, half * 4:(half + 1) * 4, 128:256], kc == 0, kc == 1, [wkv.b, ckt.b])
                            s_ = stg.get()
                            op("act", lambda e: e.activation(out=s_.t[:], in_=p.t[:], func=AF.Copy, scale=rstdkvt.t[:, tile:tile + 1]), reads=[p.b, rstdkvt.b], writes=[s_.b])
                            dma("sp", VC[tile * 128:(tile + 1) * 128, half * 512:(half + 1) * 512], s_.t[:], reads=[s_.b])
            with cx.phase():
                qn_ = Rot(cx.sbn(2, [128, S], BF16, "qn")); qr_ = Rot(cx.sbn(2, [64, S], BF16, "qr")); kn_ = Rot(cx.sbn(2, [128, S], BF16, "kn"))
                krt = cx.sb([64, S], BF16, "kr"); vr_ = Rot(cx.sbn(2, [128, NT, 128], BF16, "V"))
                dma("sp", krt.t[:], KRT[:, :], writes=[krt.b])
                Ep = Rot(cx.sbn(4, [128, 512], BF16, "E"))
                psS = Rot(PS[0:4]); psO = [PS[4], PS[5]]; psN = Rot([PS[6], PS[7]])
                accp = Rot(cx.sbn(4, [128, 512], F32, "acc"))
                rec = Rot(cx.sbn(2, [128, 512], F32, "rec")); obf = Rot(cx.sbn(2, [128, 512], BF16, "obf"))
                sc = float(192 ** -0.5)
                k_ = 0
                for h in range(8):
                    qn = qn_.get(); qr = qr_.get(); kn = kn_.get(); V = vr_.get()
                    dma("sp", qn.t[:], QNT[h * 128:(h + 1) * 128, :], writes=[qn.b])
                    dma("sp", qr.t[:], QRT[h * 64:(h + 1) * 64, :], writes=[qr.b])
                    dma("sp", kn.t[:], KNT[h * 128:(h + 1) * 128, :], writes=[kn.b])
                    dma("sp", V.t[:], VC[:, h * 128:(h + 1) * 128].rearrange("(kt p) e -> p kt e", p=128), writes=[V.b])
                    for qb in range(8):
                        maps = [[(kn, slice(0, 128), qn, slice(0, 128)), (krt, slice(0, 64), qr, slice(0, 64))]]
                        psOk = [psO[k_ % 2]]; k_ += 1
                        (o, acc_), = attn_core(maps, V, qb, psS, psOk, accp, Ep, sc)
                        r_ = rec.get(); ob_ = obf.get(); n = psN.get()
                        mm(n, n.t[:, :], onesf.t[:], acc_.t[:], True, True, [onesf.b, acc_.b])
                        op("dve", lambda e: e.reciprocal(out=r_.t[:], in_=n.t[:]), reads=[n.b], writes=[r_.b])
                        op("dve", lambda e: e.tensor_tensor(out=ob_.t[:], in0=o.t[:], in1=r_.t[:], op=ALU.mult), reads=[o.b, r_.b], writes=[ob_.b])
                        dma("sp", OCT[h * 128:(h + 1) * 128, qb * 512:(qb + 1) * 512], ob_.t[:], reads=[ob_.b])
            if "oc" in dbg and l == 0:
                DBG.update(OCT=OCT, QNT=QNT, QRT=QRT, KNT=KNT, VC=VC)
            if "stopP4" in dbg:
                break

            with cx.phase():
              rnorm = cx.sb([128, 2048], F32, "rnorm")
              with cx.phase():
                colp = cx.sb([128, 64], F32, "colp")
                dma("sp", colp.t[:], A["colp"][l], writes=[colp.b])
                zT = cx.sb([33, S], F32, "zT"); dma("sp", zT.t[:], A["zT"][:, :], writes=[zT.b])
                w1 = cx.sb([33, 64], F32, "w1"); w2 = cx.sb([64, 64], F32, "w2"); w3 = cx.sb([64, 64], F32, "w3"); wo = cx.sb([64, 4096], F32, "wo")
                dma("sp", w1.t[:], A["hy_f_w1"][l], writes=[w1.b]); dma("sp", w2.t[:], A["hy_f_w2"][l], writes=[w2.b])
                dma("sp", w3.t[:], A["hy_f_w3"][l], writes=[w3.b]); dma("sp", wo.t[:], A["hy_f_wout"][l], writes=[wo.b])
                fb = cx.sb([64, 4], F32, "fb")
                for i in range(3):
                    op("dve", lambda e, i=i: e.tensor_tensor(out=fb.t[:, i:i + 1], in0=colp.t[0:64, 55 + i:56 + i], in1=colp.t[0:64, 58:59], op=ALU.mult), reads=[colp.b], writes=[fb.b])
                hA = cx.sb([64, S], F32, "hA"); hB = cx.sb([64, S], F32, "hB")
                ti5 = cx.sb([64, 512], I32, "ti5"); tf5 = cx.sb([64, 512], F32, "tf5")
                psr = Rot(PS[4:7])
                src, K = zT, 33
                for i, (wi, dstt) in enumerate(((w1, hA), (w2, hB), (w3, hA))):
                    for tb in range(8):
                        tsl = slice(tb * 512, (tb + 1) * 512)
                        p = psr.get()
                        mm(p, p.t[0:64, :], wi.t[0:K, :], src.t[0:K, tsl], True, True, [wi.b, src.b])
                        op("dve", lambda e: e.tensor_scalar(out=dstt.t[:, tsl], in0=p.t[0:64, :], scalar1=colp.t[0:64, 58:59], scalar2=fb.t[:, i:i + 1], op0=ALU.mult, op1=ALU.add), reads=[p.b, colp.b, fb.b], writes=[dstt.b])
                        sin_inplace(dstt, dstt.t[:, tsl], ti5, tf5)
                    src, K = dstt, 64
                h3 = hA
                dec = Rot(cx.sbn(2, [128, 1024], F32, "dec")); hf = Rot(cx.sbn(2, [128, 4096], F32, "hf"))
                ab = Rot(cx.sbn(2, [128, 4096], BF16, "ab")); ab2 = Rot(cx.sbn(2, [128, 2048], BF16, "ab2"))
                spt = Rot(cx.sbn(2, [128, 2048], BF16, "spt")); smt = Rot(cx.sbn(2, [128, 2048], BF16, "smt"))
                for tt in range(NT):
                    dc = dec.get(); hft = hf.get()
                    dma("sp", dc.t[:], A["decay"][tt * 128:(tt + 1) * 128, :], writes=[dc.b])
                    for cbk in range(8):
                        p = psr.get()
                        mm(p, p.t[:, :], h3.t[:, tt * 128:(tt + 1) * 128], wo.t[:, cbk * 512:(cbk + 1) * 512], True, True, [h3.b, wo.b])
                        ch0 = (cbk % 2) * 512
                        op("dve", lambda e: e.tensor_tensor(out=hft.t[:, cbk * 512:(cbk + 1) * 512], in0=p.t[:], in1=dc.t[:, ch0:ch0 + 512], op=ALU.mult), reads=[p.b, dc.b], writes=[hft.b])
                    hv = hft.t[:].rearrange("p (o d c) -> p o d c", o=2, d=2)
                    if tt == 0:
                        op("dve", lambda e: e.tensor_tensor(out=hv[0:1, :, 0, :], in0=hv[0:1, :, 0, :], in1=hv[0:1, :, 1, :], op=ALU.add), reads=[hft.b], writes=[hft.b])
                        op("dve", lambda e: e.memset(hv[0:1, :, 1, :], 0.0), reads=[hft.b], writes=[hft.b])
                    a_ = ab.get(); a2_ = ab2.get(); sp_ = spt.get(); sm_ = smt.get()
                    op("act", lambda e: e.activation(out=a_.t[:], in_=hft.t[:], func=AF.Abs), reads=[hft.b], writes=[a_.b])
                    av = a_.t[:].rearrange("p (o d c) -> p o d c", o=2, d=2)
                    op("pool", lambda e: e.tensor_tensor(out=a2_.t[:].rearrange("p (o c) -> p o c", o=2), in0=av[:, :, 0, :], in1=av[:, :, 1, :], op=ALU.add), reads=[a_.b], writes=[a2_.b])
                    for cbk in range(4):
                        mm(PS[cbk], PS[cbk].t[:, :], ones.t[:], a2_.t[:, cbk * 512:(cbk + 1) * 512], tt == 0, tt == NT - 1, [ones.b, a2_.b])
                    op("dve", lambda e: e.tensor_tensor(out=sp_.t[:].rearrange("p (o c) -> p o c", o=2), in0=hv[:, :, 0, :], in1=hv[:, :, 1, :], op=ALU.add), reads=[hft.b], writes=[sp_.b])
                    op("pool", lambda e: e.tensor_tensor(out=sm_.t[:].rearrange("p (o c) -> p o c", o=2), in0=hv[:, :, 0, :], in1=hv[:, :, 1, :], op=ALU.subtract), reads=[hft.b], writes=[sm_.b])
                    dma("sp", SPB[tt * 128:(tt + 1) * 128, :], sp_.t[:], reads=[sp_.b])
                    dma("sp", SMB[tt * 128:(tt + 1) * 128, :], sm_.t[:], reads=[sm_.b])
                for cbk in range(4):
                    op("dve", lambda e, cbk=cbk: e.reciprocal(out=rnorm.t[:, cbk * 512:(cbk + 1) * 512], in_=PS[cbk].t[:]), reads=[PS[cbk].b], writes=[rnorm.b])
              with cx.phase():
                zs = cx.sb([128, NT, 512], BF16, "zs"); zm = cx.sb([128, NT, 512], BF16, "zm")
                wf = Rot(cx.sbn(3, [128, NT, 128], BF16, "wf")); hs = Rot(cx.sbn(3, [128, 512], F32, "hs"))
                Wc = [load_row_bc(A["hy_conv_w"][l, j:j + 1, :], 3072, "wc%d" % j) for j in range(3)]
                Bc = load_row_bc(A["hy_conv_b"][l], 3072, "bc")
                ur = [Rot(cx.sbn(2, [128, 512], F32, "u%d" % j)) for j in range(3)]
                acc6 = Rot(cx.sbn(2, [128, 512], F32, "acc")); tq6 = Rot(cx.sbn(2, [128, 512], F32, "tq")); vb6 = Rot(cx.sbn(2, [128, 512], BF16, "vb"))

                def p6_unit(tt, part, half):
                    dstf = (VF, X1F, X2F)[part]
                    csl = slice(part * 1024 + half * 512, part * 1024 + (half + 1) * 512)
                    osl = slice(half * 512, (half + 1) * 512)
                    U = []
                    for j in range(3):
                        u = ur[j].get()
                        dma("sp", u.t[:], USC[tt * 128 + j: tt * 128 + j + 128, csl], writes=[u.b])
                        U.append(u)
                    a_ = acc6.get(); t_ = tq6.get()
                    op("dve", lambda e: e.tensor_tensor(out=a_.t[:], in0=U[0].t[:], in1=Wc[0].t[:, csl], op=ALU.mult), reads=[U[0].b, Wc[0].b], writes=[a_.b])
                    op("pool", lambda e: e.tensor_tensor(out=t_.t[:], in0=U[1].t[:], in1=Wc[1].t[:, csl], op=ALU.mult), reads=[U[1].b, Wc[1].b], writes=[t_.b])
                    op("dve", lambda e: e.tensor_tensor(out=a_.t[:], in0=a_.t[:], in1=t_.t[:], op=ALU.add), reads=[a_.b, t_.b], writes=[a_.b])
                    op("pool", lambda e: e.tensor_tensor(out=t_.t[:], in0=U[2].t[:], in1=Wc[2].t[:, csl], op=ALU.mult), reads=[U[2].b, Wc[2].b], writes=[t_.b])
                    op("dve", lambda e: e.tensor_tensor(out=a_.t[:], in0=a_.t[:], in1=t_.t[:], op=ALU.add), reads=[a_.b, t_.b], writes=[a_.b])
                    op("dve", lambda e: e.tensor_tensor(out=a_.t[:], in0=a_.t[:], in1=Bc.t[:, csl], op=ALU.add), reads=[a_.b, Bc.b], writes=[a_.b])
                    dma("sp", dstf[tt * 128:(tt + 1) * 128, osl], a_.t[:], reads=[a_.b])
                    if part == 0:
                        v_ = vb6.get()
                        op("act", lambda e: e.copy(out=v_.t[:], in_=a_.t[:]), reads=[a_.b], writes=[v_.b])
                        dma("sp", VB[tt * 128:(tt + 1) * 128, osl], v_.t[:], reads=[v_.b])
                units = [(tt, part, half) for tt in range(NT) for part in range(3) for half in range(2)]
                ps2 = Rot(PS[0:4])
                for cbk in range(4):
                    csl = slice(cbk * 512, (cbk + 1) * 512)
                    dma("sp", zs.t[:], SPB[:, csl].rearrange("(tc p) c -> p tc c", p=128), writes=[zs.b])
                    dma("sp", zm.t[:], SMB[:, csl].rearrange("(tc p) c -> p tc c", p=128), writes=[zm.b])
                    for kt in range(64):
                        w = wf.get(); zz = zs if kt < 32 else zm
                        dma("sp", w.t[:], A["wft"][kt], writes=[w.b])
                        p = ps2.get()
                        for tc in range(NT):
                            mm(p, p.t[:, :], w.t[:, tc, :], zz.t[:, tc, :], tc == 0, tc == NT - 1, [w.b, zz.b])
                        h_ = hs.get()
                        op("dve", lambda e: e.tensor_tensor(out=h_.t[:], in0=p.t[:], in1=rnorm.t[:, csl], op=ALU.mult), reads=[p.b, rnorm.b], writes=[h_.b])
                        dma("sp", HSPEC[kt * 128:(kt + 1) * 128, csl], h_.t[:], reads=[h_.b])
                        if units:
                            p6_unit(*units.pop(0))
                while units:
                    p6_unit(*units.pop(0))

            if "stopP5" in dbg:
                break
            if "stopP6" in dbg:
                break
            for o_ in range(2):
                ZB_in, ZF_in, GATE = (VB, VF, X1F) if o_ == 0 else (Z1B, Z1F, X2F)
                with cx.phase():
                    skp = load_row_bc(A["hy_skip"][l, o_:o_ + 1, :], 1024, "skip")
                    z = cx.sb([128, NT, 512], BF16, "z"); Y = cx.sb([128, 64, 512], BF16, "Y")
                    wp = Rot(cx.sbn(2, [128, 64, 128], BF16, "wp"))
                    hr_ = Rot(cx.sbn(2, [128, 512], F32, "hr")); hi_ = Rot(cx.sbn(2, [128, 512], F32, "hi"))
                    m_ = [Rot(cx.sbn(1, [128, 512], F32, "m%d" % i)) for i in range(4)]
                    zf_ = Rot(cx.sbn(2, [128, 512], F32, "zf")); gt_ = Rot(cx.sbn(2, [128, 512], F32, "gt"))
                    y2_ = Rot(cx.sbn(2, [128, 512], F32, "y2")); zo_ = Rot(cx.sbn(2, [128, 512], F32, "zo")); zb_ = Rot(cx.sbn(2, [128, 512], BF16, "zb"))
                    ot_ = Rot(cx.sbn(2, [128, 4, 128], BF16, "ot"))
                    psX = Rot(PS[0:4]); psY = Rot(PS[4:7])
                    wslots = []
                    for w_ in wp.l:
                        for hf_ in range(2):
                            tl_ = Tl(w_.t); tl_.lo = hf_ * 32
                            wslots.append(tl_)
                    wsl = Rot(wslots)
                    tcount = 0
                    for cb in range(2):
                        csl = slice(cb * 512, (cb + 1) * 512)
                        hsl = slice(o_ * 1024 + cb * 512, o_ * 1024 + (cb + 1) * 512)
                        dma("sp", z.t[:], ZB_in[:, csl].rearrange("(tc p) c -> p tc c", p=128), writes=[z.b])
                        for j in range(32):
                            X = []
                            for kt in (j, 32 + j):
                                w = wsl.get()
                                dma("sp", w.t[:, w.lo:w.lo + 32, :], A["wft"][kt], writes=[w.b])
                                p = psX.get()
                                for tc in range(NT):
                                    mm(p, p.t[:, :], w.t[:, w.lo + tc, :], z.t[:, tc, :], tc == 0, tc == NT - 1, [w.b, z.b])
                                X.append(p)
                            hr = hr_.get(); hi = hi_.get()
                            dma("sp", hr.t[:], HSPEC[j * 128:(j + 1) * 128, hsl], writes=[hr.b])
                            dma("sp", hi.t[:], HSPEC[4096 + j * 128:4096 + (j + 1) * 128, hsl], writes=[hi.b])
                            m = [r.get() for r in m_]
                            op("dve", lambda e: e.tensor_tensor(out=m[0].t[:], in0=X[0].t[:], in1=hr.t[:], op=ALU.mult), reads=[X[0].b, hr.b], writes=[m[0].b])
                            op("dve", lambda e: e.tensor_tensor(out=m[1].t[:], in0=X[1].t[:], in1=hi.t[:], op=ALU.mult), reads=[X[1].b, hi.b], writes=[m[1].b])
                            op("dve", lambda e: e.tensor_tensor(out=m[2].t[:], in0=X[0].t[:], in1=hi.t[:], op=ALU.mult), reads=[X[0].b, hi.b], writes=[m[2].b])
                            op("dve", lambda e: e.tensor_tensor(out=m[3].t[:], in0=X[1].t[:], in1=hr.t[:], op=ALU.mult), reads=[X[1].b, hr.b], writes=[m[3].b])
                            op("pool", lambda e: e.tensor_tensor(out=Y.t[:, j, :], in0=m[0].t[:], in1=m[1].t[:], op=ALU.subtract), reads=[m[0].b, m[1].b], writes=[Y.b])
                            op("pool", lambda e: e.tensor_tensor(out=Y.t[:, 32 + j, :], in0=m[2].t[:], in1=m[3].t[:], op=ALU.add), reads=[m[2].b, m[3].b], writes=[Y.b])
                        cx.barrier()
                        for tt in range(NT):
                            w = wp.get()
                            dma("sp", w.t[:], A["wit"][tt], writes=[w.b])
                            p = psY.get()
                            for kc in range(64):
                                mm(p, p.t[:, :], w.t[:, kc, :], Y.t[:, kc, :], kc == 0, kc == 63, [w.b, Y.b])
                            zf = zf_.get(); gt = gt_.get(); y2 = y2_.get(); zo = zo_.get(); zb = zb_.get()
                            rsl = slice(tt * 128, (tt + 1) * 128)
                            dma("sp", zf.t[:], ZF_in[rsl, csl], writes=[zf.b])
                            dma("sp", gt.t[:], GATE[rsl, csl], writes=[gt.b])
                            op("pool", lambda e: e.tensor_tensor(out=zf.t[:], in0=zf.t[:], in1=skp.t[:, csl], op=ALU.mult), reads=[zf.b, skp.b], writes=[zf.b])
                            op("dve", lambda e: e.tensor_tensor(out=y2.t[:], in0=p.t[:], in1=zf.t[:], op=ALU.add), reads=[p.b, zf.b], writes=[y2.b])
                            if o_ == 0:
                                op("dve", lambda e: e.tensor_tensor(out=zo.t[:], in0=y2.t[:], in1=gt.t[:], op=ALU.mult), reads=[y2.b, gt.b], writes=[zo.b])
                                dma("sp", Z1F[rsl, csl], zo.t[:], reads=[zo.b])
                                op("act", lambda e: e.copy(out=zb.t[:], in_=zo.t[:]), reads=[zo.b], writes=[zb.b])
                                dma("sp", Z1B[rsl, csl], zb.t[:], reads=[zb.b])
                            else:
                                op("dve", lambda e: e.tensor_tensor(out=zo.t[:], in0=y2.t[:], in1=gt.t[:], op=ALU.mult), reads=[y2.b, gt.b], writes=[zo.b])
                                hh = PS[7]
                                for q4 in range(4):
                                    op("pe", lambda e, q4=q4: e.transpose(hh.t[:, q4 * 128:(q4 + 1) * 128], zo.t[:, q4 * 128:(q4 + 1) * 128], identf.t[:]), reads=[zo.b, identf.b], writes=[hh.b])
                                ot = ot_.get()
                                op("act", lambda e: e.copy(out=ot.t[:], in_=hh.t[:, :].rearrange("p (a b) -> p a b", a=4)), reads=[hh.b], writes=[ot.b])
                                dma("sp", OBT[cb * 512:(cb + 1) * 512, rsl].rearrange("(a p) t -> p a t", p=128), ot.t[:], reads=[ot.b])
                        cx.barrier()
            if "ob" in dbg and l == 0:
                DBG.update(OBT=OBT, HSPEC=HSPEC, VF=VF, X1F=X1F, Z1F=Z1F)
            if "stopP8" in dbg:
                break

            with cx.phase():
                WB = []
                for nm in ("w_branch_a", "w_branch_b", "w_branch_c"):
                    w = cx.sb([128, 8, D], BF16, nm)
                    for q in range(4):
                        dma("pool", w.t[:, :, q * 512:(q + 1) * 512], A[nm][l][:, q * 512:(q + 1) * 512].rearrange("(kc p) c -> p kc c", p=128), writes=[w.b])
                    WB.append(w)
                ob_ = [Rot(cx.sbn(2, [128, 8, 512], BF16, "o%d" % i)) for i in range(3)]
                gr = Rot(cx.sbn(2, [128, 3, 512], BF16, "g"))
                mA = Rot(cx.sbn(2, [128, 512], F32, "mA")); mB = Rot(cx.sbn(2, [128, 512], F32, "mB")); mC = Rot(cx.sbn(2, [128, 512], F32, "mC")); mo = Rot(cx.sbn(2, [128, 512], BF16, "mo"))
                psr = Rot(PS[0:6])
                GTv = GT.rearrange("(b r) t -> r b t", b=3)
                for ts in range(8):
                    tsl = slice(ts * 512, (ts + 1) * 512)
                    ob3 = []
                    for i, src in enumerate((OAT, OBT, OCT)):
                        t_ = ob_[i].get()
                        dma("sp", t_.t[:], src.rearrange("(kc p) t -> p kc t", p=128)[:, :, tsl], writes=[t_.b])
                        ob3.append(t_)
                    for dmc in range(16):
                        g_ = gr.get()
                        dma("sp", g_.t[:], GTv[dmc * 128:(dmc + 1) * 128, :, tsl], writes=[g_.b])
                        pp = []
                        for i in range(3):
                            p = psr.get()
                            for kc in range(8):
                                mm(p, p.t[:, :], WB[i].t[:, kc, dmc * 128:(dmc + 1) * 128], ob3[i].t[:, kc, :], kc == 0, kc == 7, [WB[i].b, ob3[i].b])
                            pp.append(p)
                        a_, b_, c_, o__ = mA.get(), mB.get(), mC.get(), mo.get()
                        op("dve", lambda e: e.tensor_tensor(out=a_.t[:], in0=pp[0].t[:], in1=g_.t[:, 0, :], op=ALU.mult), reads=[pp[0].b, g_.b], writes=[a_.b])
                        op("dve", lambda e: e.tensor_tensor(out=b_.t[:], in0=pp[1].t[:], in1=g_.t[:, 1, :], op=ALU.mult), reads=[pp[1].b, g_.b], writes=[b_.b])
                        op("dve", lambda e: e.tensor_tensor(out=c_.t[:], in0=pp[2].t[:], in1=g_.t[:, 2, :], op=ALU.mult), reads=[pp[2].b, g_.b], writes=[c_.b])
                        op("pool", lambda e: e.tensor_tensor(out=a_.t[:], in0=a_.t[:], in1=b_.t[:], op=ALU.add), reads=[a_.b, b_.b], writes=[a_.b])
                        op("pool", lambda e: e.tensor_tensor(out=o__.t[:], in0=a_.t[:], in1=c_.t[:], op=ALU.add), reads=[a_.b, c_.b], writes=[o__.b])
                        dma("sp", MT[dmc * 128:(dmc + 1) * 128, tsl], o__.t[:], reads=[o__.b])
            if "merged" in dbg and l == 0:
                DBG.update(MT=MT)
            if "stopP9" in dbg:
                break

            with cx.phase():
                Wo = cx.sbn(4, [128, 16, 512], BF16, "wo")
                for nb in range(4):
                    dma("pool", Wo[nb].t[:], A["w_out"][l][:, nb * 512:(nb + 1) * 512].rearrange("(kc p) c -> p kc c", p=128), writes=[Wo[nb].b])
                Gt = load_row_bc(A["ln1_g"][l], D, "G"); Bt = load_row_bc(A["ln1_b"][l], D, "B")
                tmp = ln_tmp(); xr = Rot(cx.sbn(2, [128, D], F32, "x")); mr = Rot(cx.sbn(2, [128, 16, 512], BF16, "mT"))
                psr = Rot(PS[0:6])
                MTv = MT.rearrange("(kc p) t -> p kc t", p=128)
                for tb in range(8):
                    mT = mr.get()
                    dma("sp", mT.t[:], MTv[:, :, tb * 512:(tb + 1) * 512], writes=[mT.b])
                    for t4 in range(4):
                        tok0 = tb * 512 + t4 * 128
                        xt = xr.get()
                        dma("sp", xt.t[:], H32[tok0:tok0 + 128, :], writes=[xt.b])
                        for nb in range(4):
                            p = psr.get()
                            for kc in range(16):
                                mm(p, p.t[:, :], mT.t[:, kc, t4 * 128:(t4 + 1) * 128], Wo[nb].t[:, kc, :], kc == 0, kc == 15, [mT.b, Wo[nb].b])
                            op("dve", lambda e, nb=nb: e.scalar_tensor_tensor(out=xt.t[:, nb * 512:(nb + 1) * 512], in0=xt.t[:, nb * 512:(nb + 1) * 512], scalar=ALPHA, in1=p.t[:], op0=ALU.mult, op1=ALU.add), reads=[xt.b, p.b], writes=[xt.b])
                        ln_tail(xt, Gt, Bt, tok0, H32, tmp)
            if "h1" in dbg and l == 0:
                DBG.update(H1=H32)
                break

            with cx.phase():
                Gt = load_row_bc(A["ln2_g"][l], D, "G"); Bt = load_row_bc(A["ln2_b"][l], D, "B")
                tmp = ln_tmp()
                wr = Rot(cx.sbn(3, [128, 16, 512], BF16, "w"))
                hb = Rot(cx.sbn(2, [128, 16, 512], BF16, "hTb"))
                aT = cx.sb([128, 32, 512], BF16, "aT")
                xs = cx.sbn(4, [128, D], F32, "x")
                rl = Rot(cx.sbn(2, [128, 512], F32, "rl"))
                psr = Rot(PS[0:2]); psa = PS[2:6]
                HTv = HT.rearrange("(kc p) t -> p kc t", p=128)
                W1 = A["mlp_w1"][l]; W2 = A["mlp_w2"][l]
                dst = OUT if l == NL - 1 else H32
                for tb in range(8):
                    hT = hb.get()
                    dma("sp", hT.t[:], HTv[:, :, tb * 512:(tb + 1) * 512], writes=[hT.b])
                    for t4 in range(4):
                        tok0 = tb * 512 + t4 * 128
                        dma("sp", xs[t4].t[:], H32[tok0:tok0 + 128, :], writes=[xs[t4].b])
                    for half in range(2):
                        for pc in range(8):
                            w = wr.get()
                            c0 = (half * 8 + pc) * 512
                            dma("pool", w.t[:], W1[:, c0:c0 + 512].rearrange("(kc p) c -> p kc c", p=128), writes=[w.b])
                            for ct in range(4):
                                p = psr.get()
                                for kc in range(16):
                                    mm(p, p.t[:, :], w.t[:, kc, ct * 128:(ct + 1) * 128], hT.t[:, kc, :], kc == 0, kc == 15, [w.b, hT.b])
                                r_ = rl.get()
                                op("act", lambda e: e.activation(out=r_.t[:], in_=p.t[:], func=AF.Relu), reads=[p.b], writes=[r_.b])
                                op("dve", lambda e: e.tensor_tensor(out=aT.t[:, pc * 4 + ct, :], in0=r_.t[:], in1=r_.t[:], op=ALU.mult), reads=[r_.b], writes=[aT.b])
                        for nb in range(4):
                            for kg in range(2):
                                w = wr.get()
                                r0 = (half * 32 + kg * 16) * 128
                                dma("pool", w.t[:], W2[r0:r0 + 2048, nb * 512:(nb + 1) * 512].rearrange("(kc p) c -> p kc c", p=128), writes=[w.b])
                                for t4 in range(4):
                                    p = psa[t4]
                                    for kc in range(16):
                                        mm(p, p.t[:, :], aT.t[:, kg * 16 + kc, t4 * 128:(t4 + 1) * 128], w.t[:, kc, :], kg == 0 and kc == 0, kg == 1 and kc == 15, [aT.b, w.b])
                            for t4 in range(4):
                                p = psa[t4]; xt = xs[t4]
                                if half == 0:
                                    op("dve", lambda e, nb=nb: e.scalar_tensor_tensor(out=xt.t[:, nb * 512:(nb + 1) * 512], in0=xt.t[:, nb * 512:(nb + 1) * 512], scalar=ALPHA, in1=p.t[:], op0=ALU.mult, op1=ALU.add), reads=[xt.b, p.b], writes=[xt.b])
                                else:
                                    op("dve", lambda e, nb=nb: e.tensor_tensor(out=xt.t[:, nb * 512:(nb + 1) * 512], in0=xt.t[:, nb * 512:(nb + 1) * 512], in1=p.t[:], op=ALU.add), reads=[xt.b, p.b], writes=[xt.b])
                    for t4 in range(4):
                        ln_tail(xs[t4], Gt, Bt, tb * 512 + t4 * 128, dst, tmp)

        if "h2" in dbg:
            DBG.update(H2=H32)
        if dbg:
            for k, apv in DBG.items():
                o = nc.dram_tensor("dbg_" + k, list(apv.shape), apv.dtype, kind="ExternalOutput").ap()
                n = apv.shape[0]
                step = max(1, (n + 7) // 8)
                for r in range(0, n, step):
                    e_ = min(n, r + step)
                    dma("sp", o[r:e_, :], apv[r:e_, :])
        cx.barrier()
    return nc


_CONST = {}


def _consts():
    if _CONST:
        return _CONST
    N = 8192
    n = np.arange(S, dtype=np.float64)[:, None]
    k = np.arange(S, dtype=np.float64)[None, :]
    th = 2.0 * np.pi * (k + 0.5) / N
    Cm = np.cos(th * n); Sm = np.sin(th * n)
    WF = np.concatenate([Cm, Sm], axis=1)
    wft = WF.reshape(32, 128, 64, 128).transpose(2, 1, 0, 3)
    WI = (2.0 / N) * WF.T
    wit = WI.reshape(64, 128, 32, 128).transpose(2, 1, 0, 3)
    _CONST["wft"] = np.ascontiguousarray(wft).astype(ml_dtypes.bfloat16)
    _CONST["wit"] = np.ascontiguousarray(wit).astype(ml_dtypes.bfloat16)
    t = np.linspace(0.0, 1.0, S, dtype=np.float32)[:, None]
    bands = 16
    w = (np.float32(2.0 * math.pi) * np.arange(S, dtype=np.float32)[:, None] / np.float32(S)).astype(np.float32)
    f = np.linspace(1e-4, bands - 1, bands, dtype=np.float32)[None, :]
    angz = (f * w).astype(np.float32)
    z = np.concatenate([t, np.cos(angz), -np.sin(angz)], axis=-1).astype(np.float32)
    deltas = np.linspace(math.log(1e-2) / 0.3, math.log(1e-2) / 1.5, 1024, dtype=np.float32)
    decay = np.exp(-t * np.abs(deltas)[None, :]).astype(np.float32)
    _CONST["zT"] = np.ascontiguousarray(z.T)
    _CONST["decay"] = np.ascontiguousarray(decay)
    cst = np.zeros((128, 8), np.float32)
    inv = (10000.0 ** (-np.arange(0, 64, 2, dtype=np.float32) / np.float32(64))).astype(np.float32)
    p = np.arange(128)
    cst[:, 0] = inv[p % 32]
    cst[:, 1] = np.where((p % 64) < 32, -1.0, 1.0)
    cst[:, 2] = -math.pi
    cst[:, 3] = LN_EPS
    cst[:, 4] = RMS_EPS
    _CONST["cst"] = cst
    cm = np.zeros((128, 256), np.float32)
    cm[p, p] = 1.0
    sw = (p // 64) * 64 + ((p % 64) + 32) % 64
    cm[sw, 128 + p] = 1.0
    _CONST["cmat"] = cm.astype(ml_dtypes.bfloat16)
    _CONST["identf"] = np.eye(128, dtype=np.float32)
    return _CONST


def make_in_maps(inputs, cores, names=None):
    f = lambda a: np.ascontiguousarray(np.asarray(a, dtype=np.float32))
    C = _consts()
    L = NL
    colp = np.zeros((L, 128, 64), np.float32)
    colp[:, :, 0:48] = f(inputs["gate_b"]).reshape(L, 48, 128).transpose(0, 2, 1)
    colp[:, :, 48:52] = f(inputs["mla_q_norm_g"]).reshape(L, 4, 128).transpose(0, 2, 1)
    colp[:, :, 52:54] = f(inputs["mla_kv_norm_g"]).reshape(L, 2, 128).transpose(0, 2, 1)
    colp[:, :, 54] = f(inputs["da_subln_g"])
    colp[:, 0:64, 55] = f(inputs["hy_f_b1"]); colp[:, 0:64, 56] = f(inputs["hy_f_b2"]); colp[:, 0:64, 57] = f(inputs["hy_f_b3"])
    colp[:, 0:64, 58] = f(inputs["hy_f_freq"])
    shared = {
        "ln_emb_g": f(inputs["ln_emb_g"]).reshape(1, D), "ln_emb_b": f(inputs["ln_emb_b"]).reshape(1, D),
        "w_in": f(inputs["w_in"]), "da_lambda": f(inputs["da_lambda"]).reshape(L, 1, 256),
        "hy_conv_w": f(inputs["hy_conv_w"]), "hy_conv_b": f(inputs["hy_conv_b"]).reshape(L, 1, 3072),
        "hy_f_w1": f(inputs["hy_f_w1"]), "hy_f_w2": f(inputs["hy_f_w2"]), "hy_f_w3": f(inputs["hy_f_w3"]),
        "hy_f_wout": f(inputs["hy_f_wout"]), "hy_skip": f(inputs["hy_skip"]),
        "mla_w_uq": f(inputs["mla_w_uq"]), "mla_w_ukv": f(inputs["mla_w_ukv"]),
        "w_branch_a": f(inputs["w_branch_a"]), "w_branch_b": f(inputs["w_branch_b"]), "w_branch_c": f(inputs["w_branch_c"]),
        "w_out": f(inputs["w_out"]),
        "ln1_g": f(inputs["ln1_g"]).reshape(L, 1, D), "ln1_b": f(inputs["ln1_b"]).reshape(L, 1, D),
        "mlp_w1": f(inputs["mlp_w1"]), "mlp_w2": f(inputs["mlp_w2"]),
        "ln2_g": f(inputs["ln2_g"]).reshape(L, 1, D), "ln2_b": f(inputs["ln2_b"]).reshape(L, 1, D),
        "colp": colp, "cst": C["cst"], "cmat": C["cmat"], "identf": C["identf"], "wft": C["wft"], "wit": C["wit"], "zT": C["zT"], "decay": C["decay"],
    }
    x = f(inputs["x"]); pos = np.ascontiguousarray(np.asarray(inputs["positions"], dtype=np.int32))
    maps = []
    for c in cores:
        b = c // 2
        m = dict(shared)
        m["x"] = x[b]; m["pos"] = pos[b].reshape(1, S)
        if names is not None:
            m = {k: v for k, v in m.items() if k in names}
        maps.append(m)
    return maps


def kernel(**inputs):
    nc = build(NL)
    cores = list(range(8))
    maps = make_in_maps(inputs, cores, set(nc._used_inputs.keys()))
    res = run_bass_kernel_spmd(nc, maps, core_ids=cores)
    out = np.stack([np.asarray(res.results[2 * b]["out"], dtype=np.float32) for b in range(4)], axis=0)
    return out
```

```python
import math
from contextlib import ExitStack, contextmanager
import numpy as np
import ml_dtypes
import concourse.bass as bass
import concourse.mybir as mybir
from concourse.bass_utils import run_bass_kernel_spmd

F32, BF16, I32 = mybir.dt.float32, mybir.dt.bfloat16, mybir.dt.int32
AF = mybir.ActivationFunctionType
ALU = mybir.AluOpType

S = 4096; D = 2048; DIN = 13120; NL = 4; NT = S // 128
O_QA, O_KA, O_VA, O_U, O_CQ, O_CKV, O_KR, O_G = 0, 1024, 2048, 3072, 6144, 6656, 6912, 6976
ALPHA = float((2 * NL) ** 0.25)
LN_EPS = 1e-5; RMS_EPS = 1e-6
PI = math.pi; TWO_PI = 2.0 * math.pi
NDS_SP = 24; NDS_POOL = 8


class Buf:
    __slots__ = ("w", "r", "excl")

    def __init__(self):
        self.w = None
        self.r = {}
        self.excl = False


class Tl:
    __slots__ = ("t", "b", "lo")

    def __init__(self, t):
        self.t = t
        self.b = Buf()


class Ctx:
    def __init__(self, nc, es):
        self.nc = nc
        self.E = {"pe": nc.tensor, "act": nc.scalar, "dve": nc.vector, "pool": nc.gpsimd, "sp": nc.sync}
        self.semobj = {}
        self.cnt = {}
        for k in self.E:
            self.semobj[k] = es.enter_context(nc.semaphore("s_" + k))
            self.cnt[k] = 0
        self.waited = {k: {} for k in self.E}
        self.dq = {"sp": [("dsp", i) for i in range(NDS_SP)], "pool": [("dpl", i) for i in range(NDS_POOL)]}
        self.dqi = {"sp": 0, "pool": 0}
        for q in self.dq:
            for key in self.dq[q]:
                self.semobj[key] = es.enter_context(nc.semaphore("%s%d" % key))
                self.cnt[key] = 0
        self.nuid = 0
        self.pending = []

    def _flush(self, only_seen=False):
        keep = []
        pend, self.pending = self.pending, []
        for it in pend:
            if only_seen and not it[3]:
                keep.append(it)
            else:
                self._dma_now("sp", it[0], it[1], it[2], ())
        self.pending = keep + self.pending

    def _hazard(self, writes):
        if self.pending and writes:
            ws = set(id(b) for b in writes)
            for it in self.pending:
                for b in it[2]:
                    if id(b) in ws:
                        self._flush()
                        return

    def uid(self, p):
        self.nuid += 1
        return "%s_%d" % (p, self.nuid)

    def _need(self, eng, pts):
        for key, c in pts:
            if c > 0 and self.waited[eng].get(key, 0) < c:
                self.E[eng].wait_ge(self.semobj[key], c)
                self.waited[eng][key] = c

    def _deps(self, eng, reads, writes):
        pts = []
        for b in reads:
            if b.w is not None and not (eng == "pe" and b.w[0] == "pe"):
                pts.append(b.w)
            if b.excl:
                for k, c in b.r.items():
                    if k != eng:
                        pts.append((k, c))
        for b in writes:
            if b.w is not None and b.w[0] != eng:
                pts.append(b.w)
            for k, c in b.r.items():
                if k != eng:
                    pts.append((k, c))
        return pts

    def op(self, eng, fn, reads=(), writes=(), inc=True):
        self._hazard(writes)
        self._need(eng, self._deps(eng, reads, writes))
        ins = fn(self.E[eng])
        if inc:
            self.cnt[eng] += 1
            ins.then_inc(self.semobj[eng], 1)
            c = self.cnt[eng]
        else:
            c = self.cnt[eng] + 1
        for b in reads:
            if b.r.get(eng, 0) < c:
                b.r[eng] = c
        for b in writes:
            b.w = (eng, c)
            b.r = {}

    def dma(self, q, out, in_, reads=(), writes=()):
        if q == "sp" and reads and not writes:
            self._flush(only_seen=True)
            self.pending.append([out, in_, list(reads), False])
            return
        self._hazard(writes)
        self._dma_now(q, out, in_, reads, writes)
        if q == "sp":
            for it in self.pending:
                it[3] = True

    def _dma_now(self, q, out, in_, reads=(), writes=()):
        lst = self.dq[q]
        key = lst[self.dqi[q]]
        self.dqi[q] = (self.dqi[q] + 1) % len(lst)
        pts = self._deps(q, reads, writes)
        pts.append((key, self.cnt[key]))
        self._need(q, pts)
        self.E[q].dma_start(out=out, in_=in_).then_inc(self.semobj[key], 16)
        self.cnt[key] += 16
        c = self.cnt[key]
        for b in reads:
            b.r[key] = c
        for b in writes:
            b.w = (key, c)
            b.r = {}

    def barrier(self):
        self._flush()
        for e in self.E:
            self._need(e, [(k, c) for k, c in self.cnt.items() if k != e])

    @contextmanager
    def phase(self):
        prev = getattr(self, "pes", None)
        with ExitStack() as es:
            self.pes = es
            yield es
            self.barrier()
        self.pes = prev

    def sb(self, shape, dt, name="t"):
        return Tl(self.pes.enter_context(self.nc.sbuf_tensor(self.uid(name), list(shape), dt)))

    def sbn(self, n, shape, dt, name="t"):
        return [self.sb(shape, dt, name) for _ in range(n)]


class Rot:
    def __init__(self, lst):
        self.l = lst
        self.i = 0

    def get(self):
        t = self.l[self.i]
        self.i = (self.i + 1) % len(self.l)
        return t


def build(n_layers=NL, dbg=()):
    nc = bass.Bass("TRN2", target_bir_lowering=False)

    def din(name, shape, dt=F32):
        return nc.dram_tensor(name, list(shape), dt, kind="ExternalInput").ap()

    def dsc(name, shape, dt=F32):
        return nc.dram_tensor(name, list(shape), dt, kind="Internal").ap()

    SHAPES = {
        "x": ([S, D], F32), "pos": ([1, S], I32), "ln_emb_g": ([1, D], F32), "ln_emb_b": ([1, D], F32),
        "w_in": ([NL, D, DIN], F32), "da_lambda": ([NL, 1, 256], F32),
        "hy_conv_w": ([NL, 3, 3072], F32), "hy_conv_b": ([NL, 1, 3072], F32),
        "hy_f_w1": ([NL, 33, 64], F32), "hy_f_w2": ([NL, 64, 64], F32), "hy_f_w3": ([NL, 64, 64], F32),
        "hy_f_wout": ([NL, 64, 4096], F32), "hy_skip": ([NL, 2, 1024], F32),
        "mla_w_uq": ([NL, 512, 1536], F32), "mla_w_ukv": ([NL, 256, 2048], F32),
        "w_branch_a": ([NL, 1024, D], F32), "w_branch_b": ([NL, 1024, D], F32), "w_branch_c": ([NL, 1024, D], F32),
        "w_out": ([NL, D, D], F32), "ln1_g": ([NL, 1, D], F32), "ln1_b": ([NL, 1, D], F32),
        "mlp_w1": ([NL, D, 4 * D], F32), "mlp_w2": ([NL, 4 * D, D], F32), "ln2_g": ([NL, 1, D], F32), "ln2_b": ([NL, 1, D], F32),
        "colp": ([NL, 128, 64], F32), "cst": ([128, 8], F32), "cmat": ([128, 256], BF16), "identf": ([128, 128], F32),
        "wft": ([64, 128, 32, 128], BF16), "wit": ([32, 128, 64, 128], BF16), "zT": ([33, S], F32), "decay": ([S, 1024], F32),
    }

    class LazyIn(dict):
        def __missing__(self, k):
            shp, dt = SHAPES[k]
            v = din(k, shp, dt)
            self[k] = v
            return v
    A = LazyIn()
    nc._used_inputs = A
    OUT = nc.dram_tensor("out", [S, D], F32, kind="ExternalOutput").ap()

    H32 = dsc("H32", [S, D]); HT = dsc("HT", [D, S], BF16)
    QT = dsc("QT", [1024, S], BF16); KT = dsc("KT", [1024, S], BF16); VA = dsc("VA", [S, 1024], BF16)
    USC = dsc("USC", [S + 2, 3072])
    CQG = dsc("CQG", [512, S], BF16); CKVG = dsc("CKVG", [256, S], BF16); KRT = dsc("KRT", [64, S], BF16)
    RSTDQ = dsc("RSTDQ", [128, S]); RSTDKV = dsc("RSTDKV", [128, S])
    GT = dsc("GT", [3 * D, S], BF16)
    QNT = dsc("QNT", [1024, S], BF16); QRT = dsc("QRT", [512, S], BF16); KNT = dsc("KNT", [1024, S], BF16); VC = dsc("VC", [S, 1024], BF16)
    OAT = dsc("OAT", [1024, S], BF16); OBT = dsc("OBT", [1024, S], BF16); OCT = dsc("OCT", [1024, S], BF16)
    VB = dsc("VB", [S, 1024], BF16); VF = dsc("VF", [S, 1024]); X1F = dsc("X1F", [S, 1024]); X2F = dsc("X2F", [S, 1024])
    Z1B = dsc("Z1B", [S, 1024], BF16); Z1F = dsc("Z1F", [S, 1024])
    HSPEC = dsc("HSPEC", [8192, 2048]); SPB = dsc("SPB", [S, 2048], BF16); SMB = dsc("SMB", [S, 2048], BF16)
    MT = dsc("MT", [D, S], BF16)
    ROPEC = dsc("ROPEC", [128, S]); ROPES = dsc("ROPES", [128, S])
    DBG = {}

    with ExitStack() as ges:
        cx = Ctx(nc, ges)
        op = cx.op; dma = cx.dma
        PS = [Tl(ges.enter_context(nc.psum_tensor("ps%d" % i, [128, 512], F32))) for i in range(8)]
        PST = [PS[6], PS[7]]
        for p_ in PS:
            p_.b.excl = True
        cx.pes = ges
        cst = cx.sb([128, 8], F32, "cst")
        cmat = cx.sb([128, 256], BF16, "cmat")
        ones = cx.sb([128, 128], BF16, "ones")
        rstdkvt = cx.sb([128, NT], F32, "rstdkvt")
        dma("sp", cst.t[:], A["cst"][:, :], writes=[cst.b])
        dma("sp", cmat.t[:], A["cmat"][:, :], writes=[cmat.b])
        op("dve", lambda e: e.memset(ones.t[:], 1.0), writes=[ones.b])
        identf = cx.sb([128, 128], F32, "identf")
        dma("sp", identf.t[:], A["identf"][:, :], writes=[identf.b])

        def mm(ps, psap, lhsT, rhs, start, stop, reads):
            op("pe", lambda e: e.matmul(psap, lhsT=lhsT, rhs=rhs, start=start, stop=stop), reads=reads, writes=[ps.b], inc=stop)

        PI_LO = 3.1415925

        def rsqrt(tl, ap_out, src_tl, ap_in, scale, eps_col):
            np_ = ap_out.shape[0]
            op("act", lambda e: e.activation(out=ap_out, in_=ap_in, func=AF.Sqrt, bias=cst.t[0:np_, eps_col:eps_col + 1], scale=scale), reads=[src_tl.b, cst.b], writes=[tl.b])
            op("dve", lambda e: e.reciprocal(out=ap_out, in_=ap_out), reads=[tl.b], writes=[tl.b])

        def sin_inplace(tl, ap, ti, tf):
            np_ = ap.shape[0]; nf_ = ap.shape[1]
            tiv = ti.t[0:np_, 0:nf_]; tfv = tf.t[0:np_, 0:nf_]
            op("dve", lambda e: e.tensor_scalar(out=tiv, in0=ap, scalar1=1.0 / TWO_PI, scalar2=None, op0=ALU.mult), reads=[tl.b], writes=[ti.b])
            op("dve", lambda e: e.tensor_copy(out=tfv, in_=tiv), reads=[ti.b], writes=[tf.b])
            op("dve", lambda e: e.scalar_tensor_tensor(out=ap, in0=tfv, scalar=-TWO_PI, in1=ap, op0=ALU.mult, op1=ALU.add), reads=[tf.b, tl.b], writes=[tl.b])
            op("dve", lambda e: e.tensor_scalar(out=tfv, in0=ap, scalar1=PI, scalar2=-TWO_PI, op0=ALU.is_gt, op1=ALU.mult), reads=[tl.b], writes=[tf.b])
            op("dve", lambda e: e.tensor_tensor(out=ap, in0=ap, in1=tfv, op=ALU.add), reads=[tl.b, tf.b], writes=[tl.b])
            op("dve", lambda e: e.tensor_scalar(out=tfv, in0=ap, scalar1=-PI, scalar2=TWO_PI, op0=ALU.is_lt, op1=ALU.mult), reads=[tl.b], writes=[tf.b])
            op("dve", lambda e: e.tensor_tensor(out=ap, in0=ap, in1=tfv, op=ALU.add), reads=[tl.b, tf.b], writes=[tl.b])
            op("dve", lambda e: e.tensor_scalar(out=ap, in0=ap, scalar1=PI_LO, scalar2=-PI_LO, op0=ALU.min, op1=ALU.max), reads=[tl.b], writes=[tl.b])
            op("act", lambda e: e.activation(out=ap, in_=ap, func=AF.Sin), reads=[tl.b], writes=[tl.b])

        LNL = 9
        for d_ in dbg:
            if d_.startswith("lnlvl"):
                LNL = int(d_[5:])

        def ln_tail(xt, Gt, Bt, tok0, dst, tmp):
            stats, mv, rs, yb, hst = tmp["stats"].get(), tmp["mv"].get(), tmp["rs"].get(), tmp["yb"].get(), tmp["hst"].get()
            if LNL >= 2:
                for c in range(4):
                    op("dve", lambda e, c=c: e.bn_stats(out=stats.t[:, c * 6:(c + 1) * 6], in_=xt.t[:, c * 512:(c + 1) * 512]), reads=[xt.b], writes=[stats.b])
                op("dve", lambda e: e.bn_aggr(out=mv.t[:, 0:2], in_=stats.t[:, 0:24]), reads=[stats.b], writes=[mv.b])
            if LNL >= 3:
                rsqrt(rs, rs.t[:, 0:1], mv, mv.t[:, 1:2], 1.0, 3)
                op("dve", lambda e: e.scalar_tensor_tensor(out=rs.t[:, 1:2], in0=mv.t[:, 0:1], scalar=-1.0, in1=rs.t[:, 0:1], op0=ALU.mult, op1=ALU.mult), reads=[mv.b, rs.b], writes=[rs.b])
            if LNL >= 4:
                op("act", lambda e: e.activation(out=xt.t[:], in_=xt.t[:], func=AF.Identity, scale=rs.t[:, 0:1], bias=rs.t[:, 1:2]), reads=[xt.b, rs.b], writes=[xt.b])
                op("dve", lambda e: e.tensor_tensor(out=xt.t[:], in0=xt.t[:], in1=Gt.t[:], op=ALU.mult), reads=[xt.b, Gt.b], writes=[xt.b])
                op("dve", lambda e: e.tensor_tensor(out=xt.t[:], in0=xt.t[:], in1=Bt.t[:], op=ALU.add), reads=[xt.b, Bt.b], writes=[xt.b])
            dma("sp", dst[tok0:tok0 + 128, :], xt.t[:], reads=[xt.b])
            if LNL >= 6:
                for g in range(4):
                    h = PST[g % 2]
                    for j in range(4):
                        kc = g * 4 + j
                        op("pe", lambda e, kc=kc, j=j: e.transpose(h.t[:, j * 128:(j + 1) * 128], xt.t[:, kc * 128:(kc + 1) * 128], identf.t[:]),
                           reads=[xt.b, identf.b], writes=[h.b])
                    if LNL >= 7:
                        if g % 2 == 0:
                            op("dve", lambda e, g=g: e.tensor_copy(out=hst.t[:, g * 4:(g + 1) * 4, :], in_=h.t[:, :].rearrange("p (a b) -> p a b", a=4)), reads=[h.b], writes=[hst.b])
                        else:
                            op("act", lambda e, g=g: e.copy(out=hst.t[:, g * 4:(g + 1) * 4, :], in_=h.t[:, :].rearrange("p (a b) -> p a b", a=4)), reads=[h.b], writes=[hst.b])
            if LNL >= 8:
                dma("sp", HT.rearrange("(kc p) t -> p kc t", p=128)[:, :, tok0:tok0 + 128], hst.t[:], reads=[hst.b])

        def ln_tmp():
            return {"stats": Rot(cx.sbn(2, [128, 24], F32, "st")), "mv": Rot(cx.sbn(2, [128, 2], F32, "mv")),
                    "rs": Rot(cx.sbn(2, [128, 2], F32, "rs")), "yb": Rot([None, None]),
                    "hst": Rot(cx.sbn(2, [128, 16, 128], BF16, "hst"))}

        def load_row_bc(ap_row, n, name):
            t = cx.sb([128, n], F32, name)
            dma("sp", t.t[:], ap_row.partition_broadcast(128), writes=[t.b])
            return t

        with cx.phase():
            posi = cx.sb([128, 1024], I32, "posi"); ang = cx.sb([128, 1024], F32, "ang"); a2 = cx.sb([128, 1024], F32, "a2"); tfr = cx.sb([128, 1024], F32, "tfr")
            zr = cx.sb([1, 3072], F32, "zr")
            op("dve", lambda e: e.memset(zr.t[:], 0.0), writes=[zr.b])
            dma("sp", USC[0:1, :], zr.t[:], reads=[zr.b])
            dma("sp", USC[S + 1:S + 2, :], zr.t[:], reads=[zr.b])
            for c in range(0 if "skiprope" in dbg else 4):
                sl = slice(c * 1024, (c + 1) * 1024)
                dma("sp", posi.t[:], A["pos"][0:1, sl].partition_broadcast(128), writes=[posi.b])
                op("dve", lambda e: e.tensor_copy(out=ang.t[:], in_=posi.t[:]), reads=[posi.b], writes=[ang.b])
                op("dve", lambda e: e.tensor_scalar(out=ang.t[:], in0=ang.t[:], scalar1=cst.t[:, 0:1], scalar2=None, op0=ALU.mult), reads=[ang.b, cst.b], writes=[ang.b])
                op("dve", lambda e: e.tensor_scalar(out=a2.t[:], in0=ang.t[:], scalar1=0.5 * PI, scalar2=None, op0=ALU.add), reads=[ang.b], writes=[a2.b])
                sin_inplace(ang, ang.t[:], posi, tfr)
                op("dve", lambda e: e.tensor_scalar(out=ang.t[:], in0=ang.t[:], scalar1=cst.t[:, 1:2], scalar2=None, op0=ALU.mult), reads=[ang.b, cst.b], writes=[ang.b])
                dma("sp", ROPES[:, sl], ang.t[:], reads=[ang.b])
                sin_inplace(a2, a2.t[:], posi, tfr)
                dma("sp", ROPEC[:, sl], a2.t[:], reads=[a2.b])
            Gt = load_row_bc(A["ln_emb_g"][0:1, :], D, "G"); Bt = load_row_bc(A["ln_emb_b"][0:1, :], D, "B")
            tmp = ln_tmp(); xr = Rot(cx.sbn(2, [128, D], F32, "x"))
            for tt in range(0 if "skipln" in dbg else NT):
                xt = xr.get()
                dma("sp", xt.t[:], A["x"][tt * 128:(tt + 1) * 128, :], writes=[xt.b])
                ln_tail(xt, Gt, Bt, tt * 128, H32, tmp)

        if "p0" in dbg:
            DBG.update(H32=H32, HT=HT, ROPEC=ROPEC, ROPES=ROPES)
        if "p0ln" in dbg:
            DBG.update(H32=H32, HT=HT)

        def rope_epilogue(ps, npart, Ct, St, toks_rel, dst_ap, tp, pre_scale=None):
            xb, t1, t2, ob, ps2 = tp["xb"].get(), tp["t1"].get(), tp["t2"].get(), tp["ob"].get(), tp["ps2"].get()
            P = slice(0, npart)
            src = ps
            if pre_scale is not None:
                xs = tp["xs"].get()
                op("dve", lambda e: e.tensor_tensor(out=xs.t[P, :], in0=ps.t[P, :], in1=pre_scale.t[P, :], op=ALU.mult), reads=[ps.b, pre_scale.b], writes=[xs.b])
                src = xs
            op("act", lambda e: e.copy(out=xb.t[P, :], in_=src.t[P, :]), reads=[src.b], writes=[xb.b])
            mm(ps2, ps2.t[P, :], cmat.t[P, 128:128 + npart], xb.t[P, :], True, True, [cmat.b, xb.b])
            op("dve", lambda e: e.tensor_tensor(out=t1.t[P, :], in0=src.t[P, :], in1=Ct.t[P, toks_rel], op=ALU.mult), reads=[src.b, Ct.b], writes=[t1.b])
            op("dve", lambda e: e.tensor_tensor(out=t2.t[P, :], in0=ps2.t[P, :], in1=St.t[P, toks_rel], op=ALU.mult), reads=[ps2.b, St.b], writes=[t2.b])
            op("dve", lambda e: e.tensor_tensor(out=ob.t[P, :], in0=t1.t[P, :], in1=t2.t[P, :], op=ALU.add), reads=[t1.b, t2.b], writes=[ob.b])
            dma("sp", dst_ap, ob.t[P, :], reads=[ob.b])

        def attn_core(maps, V, qb, psS, psO, psN, Epool, scale):
            nm = len(maps)
            outs = [(psO[mi % len(psO)], psN[mi % len(psN)]) for mi in range(nm)]
            sq = {}

            def score(kt):
                for mi, terms in enumerate(maps):
                    p = psS.get()
                    for ti, (kTl, ksl, qTl, qsl) in enumerate(terms):
                        mm(p, p.t[:, :], kTl.t[ksl, kt * 128:(kt + 1) * 128], qTl.t[qsl, qb * 512:(qb + 1) * 512], ti == 0, ti == len(terms) - 1, [kTl.b, qTl.b])
                    sq[(kt, mi)] = p
            la = 2 if nm == 1 else 1
            for k0 in range(la):
                score(k0)
            for kt in range(NT):
                Es = []
                for mi in range(nm):
                    p = sq.pop((kt, mi)); Et = Epool.get()
                    op("act", lambda e: e.activation(out=Et.t[:], in_=p.t[:], func=AF.Exp, scale=scale), reads=[p.b], writes=[Et.b])
                    Es.append(Et)
                if kt + la < NT:
                    score(kt + la)
                for mi in range(nm):
                    o, n = outs[mi]
                    mm(o, o.t[:, :], V.t[:, kt, :], Es[mi].t[:], kt == 0, kt == NT - 1, [V.b, Es[mi].b])
                    mm(n, n.t[:, :], ones.t[:], Es[mi].t[:], kt == 0, kt == NT - 1, [ones.b, Es[mi].b])
            return outs

        for l in range(n_layers):
            lam_init = 0.8 - 0.6 * math.exp(-0.3 * l)
            with cx.phase():
                colp = cx.sb([128, 64], F32, "colp")
                dma("sp", colp.t[:], A["colp"][l], writes=[colp.b])
                wr = Rot(cx.sbn(3, [128, 16, 512], BF16, "w"))
                hb = Rot(cx.sbn(1, [128, 16, 1024], BF16, "hTb"))
                Ct = cx.sb([128, 1024], F32, "C"); St = cx.sb([128, 1024], F32, "S")
                tp = {k: Rot(cx.sbn(2, [128, 512], dt, k)) for k, dt in (("xb", BF16), ("t1", F32), ("t2", F32), ("ob", BF16))}
                tp["ps2"] = Rot([PS[5], PS[6]])
                psr = Rot(PS[0:4])
                stg = Rot(cx.sbn(3, [128, 512], BF16, "stg")); stf = Rot(cx.sbn(2, [128, 512], F32, "stf"))
                sqr = Rot(cx.sbn(4, [128, 512], BF16, "sq")); rsb = Rot(cx.sbn(2, [128, 512], F32, "rsb")); rst = Rot(cx.sbn(2, [128, 4], F32, "rst"))
                HTv = HT.rearrange("(kc p) t -> p kc t", p=128)
                Wl = A["w_in"][l]
                P1G = set(["qk", "vu", "cq", "ckv", "g"])
                for d_ in dbg:
                    if d_.startswith("p1:"):
                        P1G = set(d_[3:].split("+"))
                for tb in range(1 if "tb1" in dbg else 4):
                    t0 = tb * 1024
                    hT = hb.get()
                    dma("sp", hT.t[:], HTv[:, :, t0:t0 + 1024], writes=[hT.b])
                    dma("sp", Ct.t[:], ROPEC[:, t0:t0 + 1024], writes=[Ct.b])
                    dma("sp", St.t[:], ROPES[:, t0:t0 + 1024], writes=[St.b])

                    def loadw(c0, ncol):
                        w = wr.get()
                        dma("pool", w.t[:, :, 0:ncol], Wl[:, c0:c0 + ncol].rearrange("(kc p) c -> p kc c", p=128), writes=[w.b])
                        return w

                    def fproj(w, ci, m, ts):
                        p = psr.get()
                        for kc in range(16):
                            mm(p, p.t[0:m, :], w.t[:, kc, ci:ci + m], hT.t[:, kc, ts * 512:(ts + 1) * 512], kc == 0, kc == 15, [w.b, hT.b])
                        return p
                    for (c0, dst) in ((O_QA, QT), (O_QA + 512, QT), (O_KA, KT), (O_KA + 512, KT)) if "qk" in P1G else ():
                        w = loadw(c0, 512)
                        r0 = (c0 - O_QA) % 1024
                        for ts in range(2):
                            for ct in range(4):
                                p = fproj(w, ct * 128, 128, ts)
                                rope_epilogue(p, 128, Ct, St, slice(ts * 512, (ts + 1) * 512), dst[r0 + ct * 128: r0 + (ct + 1) * 128, t0 + ts * 512: t0 + (ts + 1) * 512], tp)
                    for (c0, kind) in ([(O_VA, "va"), (O_VA + 512, "va")] + [(O_U + i * 512, "u") for i in range(6)]) if "vu" in P1G else ():
                        w = loadw(c0, 512)
                        for tt in range(8):
                            p = psr.get()
                            for kc in range(16):
                                mm(p, p.t[:, :], hT.t[:, kc, tt * 128:(tt + 1) * 128], w.t[:, kc, :], kc == 0, kc == 15, [w.b, hT.b])
                            tok = t0 + tt * 128
                            if kind == "va":
                                s_ = stg.get()
                                op("act", lambda e: e.copy(out=s_.t[:], in_=p.t[:]), reads=[p.b], writes=[s_.b])
                                dma("sp", VA[tok:tok + 128, c0 - O_VA:c0 - O_VA + 512], s_.t[:], reads=[s_.b])
                            else:
                                s_ = stf.get()
                                op("dve", lambda e: e.tensor_copy(out=s_.t[:], in_=p.t[:]), reads=[p.b], writes=[s_.b])
                                dma("sp", USC[1 + tok:1 + tok + 128, c0 - O_U:c0 - O_U + 512], s_.t[:], reads=[s_.b])
                    for (c0, nct, gcol, nfeat, dstg, dstr) in ([(O_CQ, 4, 48, 512.0, CQG, RSTDQ)] if "cq" in P1G else []) + ([(O_CKV, 2, 52, 256.0, CKVG, RSTDKV)] if "ckv" in P1G else []):
                        ncol = 512 if nct == 4 else 320
                        w = loadw(c0, ncol)
                        for ts in range(2):
                            tsl = slice(t0 + ts * 512, t0 + (ts + 1) * 512)
                            sqs = []
                            pN = PS[4]
                            for ct in range(nct):
                                p = fproj(w, ct * 128, 128, ts)
                                sq = sqr.get(); s_ = stg.get()
                                op("act", lambda e: e.activation(out=sq.t[:], in_=p.t[:], func=AF.Square), reads=[p.b], writes=[sq.b])
                                op("dve", lambda e: e.tensor_scalar(out=s_.t[:], in0=p.t[:], scalar1=colp.t[:, gcol + ct:gcol + ct + 1], scalar2=None, op0=ALU.mult), reads=[p.b, colp.b], writes=[s_.b])
                                dma("sp", dstg[ct * 128:(ct + 1) * 128, tsl], s_.t[:], reads=[s_.b])
                                mm(pN, pN.t[:, :], ones.t[:], sq.t[:], ct == 0, ct == nct - 1, [ones.b, sq.b])
                                sqs.append(sq)
                            r_ = rsb.get()
                            rsqrt(r_, r_.t[:], pN, pN.t[:], 1.0 / nfeat, 4)
                            dma("sp", dstr[:, tsl], r_.t[:], reads=[r_.b])
                            if nct == 2:
                                pT = PS[4]
                                for t4 in range(4):
                                    for ct in range(2):
                                        mm(pT, pT.t[:, t4:t4 + 1], sqs[ct].t[:, t4 * 128:(t4 + 1) * 128], ones.t[:, 0:1], ct == 0, ct == 1, [sqs[ct].b, ones.b])
                                tile0 = (t0 + ts * 512) // 128
                                r4 = rst.get()
                                rsqrt(rstdkvt, rstdkvt.t[:, tile0:tile0 + 4], pT, pT.t[:, 0:4], 1.0 / nfeat, 4)
                                p = fproj(w, 256, 64, ts)
                                rope_epilogue(p, 64, Ct, St, slice(ts * 512, (ts + 1) * 512), KRT[:, tsl], tp)
                    for gi in range(12 if "g" in P1G else 0):
                        w = loadw(O_G + gi * 512, 512)
                        for ts in range(2):
                            for ct in range(4):
                                p = fproj(w, ct * 128, 128, ts)
                                s_ = stg.get(); j = gi * 4 + ct
                                op("act", lambda e: e.activation(out=s_.t[:], in_=p.t[:], func=AF.Sigmoid, bias=colp.t[:, j:j + 1], scale=1.0), reads=[p.b, colp.b], writes=[s_.b])
                                dma("sp", GT[j * 128:(j + 1) * 128, t0 + ts * 512:t0 + (ts + 1) * 512], s_.t[:], reads=[s_.b])

            if "proj" in dbg and l == 0:
                DBG.update(QT=QT, KT=KT, VA=VA, USC=USC, CQG=CQG, CKVG=CKVG, KRT=KRT, GT=GT, RSTDQ=RSTDQ, HT=HT, H32=H32)
            if "projx" in dbg and l == 0:
                if "qk" in P1G: DBG.update(QT=QT, KT=KT)
                if "vu" in P1G: DBG.update(VA=VA, USC=USC)
                if "cq" in P1G: DBG.update(CQG=CQG, RSTDQ=RSTDQ)
                if "ckv" in P1G: DBG.update(CKVG=CKVG, KRT=KRT, RSTDKV=RSTDKV)
                if "g" in P1G: DBG.update(GT=GT)
            if "stopP1" in dbg:
                break

            with cx.phase():
                colp = cx.sb([128, 64], F32, "colp")
                dma("sp", colp.t[:], A["colp"][l], writes=[colp.b])
                lamt = load_row_bc(A["da_lambda"][l], 256, "lam")
                lw = cx.sb([128, 8], F32, "lw"); lp = cx.sb([128, 128], F32, "lp")
                op("dve", lambda e: e.tensor_tensor(out=lp.t[:, 0:64], in0=lamt.t[:, 0:64], in1=lamt.t[:, 64:128], op=ALU.mult), reads=[lamt.b], writes=[lp.b])
                op("dve", lambda e: e.tensor_tensor(out=lp.t[:, 64:128], in0=lamt.t[:, 128:192], in1=lamt.t[:, 192:256], op=ALU.mult), reads=[lamt.b], writes=[lp.b])
                op("dve", lambda e: e.reduce_sum(out=lw.t[:, 0:2], in_=lp.t[:].rearrange("p (a b) -> p a b", a=2), axis=mybir.AxisListType.X), reads=[lp.b], writes=[lw.b])
                op("act", lambda e: e.activation(out=lw.t[:, 2:4], in_=lw.t[:, 0:2], func=AF.Exp), reads=[lw.b], writes=[lw.b])
                op("dve", lambda e: e.tensor_tensor(out=lw.t[:, 4:5], in0=lw.t[:, 3:4], in1=lw.t[:, 2:3], op=ALU.subtract), reads=[lw.b], writes=[lw.b])
                op("dve", lambda e: e.tensor_scalar(out=lw.t[:, 5:6], in0=lw.t[:, 4:5], scalar1=-lam_init, scalar2=None, op0=ALU.add), reads=[lw.b], writes=[lw.b])
                op("dve", lambda e: e.tensor_scalar(out=lw.t[:, 6:7], in0=colp.t[:, 54:55], scalar1=1.0 - lam_init, scalar2=None, op0=ALU.mult), reads=[colp.b], writes=[lw.b])
                qr_ = Rot(cx.sbn(2, [128, S], BF16, "qT")); kr_ = Rot(cx.sbn(2, [128, S], BF16, "kT")); vr_ = Rot(cx.sbn(2, [128, NT, 128], BF16, "V"))
                Ep = Rot(cx.sbn(6, [128, 512], BF16, "E"))
                psS = Rot([PS[0], PS[1], PS[2], PS[7]]); psO = [PS[3], PS[4]]; psN = [PS[5], PS[6]]
                rec = Rot(cx.sbn(2, [128, 512], F32, "rec")); oc = Rot(cx.sbn(4, [128, 512], F32, "oc"))
                sqb = Rot(cx.sbn(2, [128, 512], BF16, "sqb")); obf = Rot(cx.sbn(2, [128, 512], BF16, "obf"))
                for h in range(8):
                    qT = qr_.get(); kT = kr_.get(); V = vr_.get()
                    dma("sp", qT.t[:], QT[h * 128:(h + 1) * 128, :], writes=[qT.b])
                    dma("sp", kT.t[:], KT[h * 128:(h + 1) * 128, :], writes=[kT.b])
                    dma("sp", V.t[:], VA[:, h * 128:(h + 1) * 128].rearrange("(kt p) e -> p kt e", p=128), writes=[V.b])
                    for qb in range(8):
                        maps = [[(kT, slice(c * 64, (c + 1) * 64), qT, slice(c * 64, (c + 1) * 64))] for c in range(2)]
                        res = attn_core(maps, V, qb, psS, psO, psN, Ep, 0.125)
                        ocs = []
                        for (o, n) in res:
                            r_ = rec.get(); oo = oc.get()
                            op("dve", lambda e: e.reciprocal(out=r_.t[:], in_=n.t[:]), reads=[n.b], writes=[r_.b])
                            op("dve", lambda e: e.tensor_tensor(out=oo.t[:], in0=o.t[:], in1=r_.t[:], op=ALU.mult), reads=[o.b, r_.b], writes=[oo.b])
                            ocs.append(oo)
                        od = oc.get()
                        op("dve", lambda e: e.scalar_tensor_tensor(out=od.t[:], in0=ocs[1].t[:], scalar=lw.t[:, 5:6], in1=ocs[0].t[:], op0=ALU.mult, op1=ALU.add), reads=[ocs[0].b, ocs[1].b, lw.b], writes=[od.b])
                        sq = sqb.get(); pR = psS.get(); r_ = rec.get(); ob_ = obf.get()
                        op("act", lambda e: e.activation(out=sq.t[:], in_=od.t[:], func=AF.Square), reads=[od.b], writes=[sq.b])
                        mm(pR, pR.t[:, :], ones.t[:], sq.t[:], True, True, [ones.b, sq.b])
                        rsqrt(r_, r_.t[:], pR, pR.t[:], 1.0 / 128.0, 3)
                        op("dve", lambda e: e.scalar_tensor_tensor(out=ob_.t[:], in0=od.t[:], scalar=lw.t[:, 6:7], in1=r_.t[:], op0=ALU.mult, op1=ALU.mult), reads=[od.b, lw.b, r_.b], writes=[ob_.b])
                        dma("sp", OAT[h * 128:(h + 1) * 128, qb * 512:(qb + 1) * 512], ob_.t[:], reads=[ob_.b])
            if "oa" in dbg and l == 0:
                DBG.update(OAT=OAT)
            if "stopP2" in dbg:
                break

            with cx.phase():
                wqn = cx.sb([128, 4, 8, 128], BF16, "wqn"); wqr = cx.sb([128, 4, 8, 64], BF16, "wqr"); wkv = cx.sb([128, 2, 8, 256], BF16, "wkv")
                wq_v = A["mla_w_uq"][l].rearrange("(kc p) (h x) -> p kc h x", p=128, x=192)
                for kc in range(4):
                    dma("pool", wqn.t[:, kc], wq_v[:, kc, :, 0:128], writes=[wqn.b])
                    dma("pool", wqr.t[:, kc], wq_v[:, kc, :, 128:192], writes=[wqr.b])
                wkv_v = A["mla_w_ukv"][l].rearrange("(kc p) (h x) -> p kc h x", p=128, x=256)
                for kc in range(2):
                    dma("pool", wkv.t[:, kc], wkv_v[:, kc], writes=[wkv.b])
                cq = Rot(cx.sbn(2, [128, 4, 512], BF16, "cq")); ckv = Rot(cx.sbn(2, [128, 2, 512], BF16, "ckv"))
                rq = Rot(cx.sbn(2, [128, 512], F32, "rq")); rk = Rot(cx.sbn(2, [128, 512], F32, "rk"))
                Cr = Rot(cx.sbn(2, [128, 512], F32, "C")); Sr = Rot(cx.sbn(2, [128, 512], F32, "S"))
                tp = {k: Rot(cx.sbn(2, [128, 512], dt, k)) for k, dt in (("xb", BF16), ("t1", F32), ("t2", F32), ("ob", BF16), ("xs", F32))}
                tp["ps2"] = Rot([PS[5], PS[6]])
                psr = Rot(PS[0:5])
                stg = Rot(cx.sbn(3, [128, 512], BF16, "stg"))
                for ts in range(8):
                    tsl = slice(ts * 512, (ts + 1) * 512)
                    cqt = cq.get(); ckt = ckv.get(); rqt = rq.get(); rkt = rk.get(); Ct = Cr.get(); St = Sr.get()
                    dma("sp", cqt.t[:], CQG.rearrange("(kc p) t -> p kc t", p=128)[:, :, tsl], writes=[cqt.b])
                    dma("sp", ckt.t[:], CKVG.rearrange("(kc p) t -> p kc t", p=128)[:, :, tsl], writes=[ckt.b])
                    dma("sp", rqt.t[:], RSTDQ[:, tsl], writes=[rqt.b]); dma("sp", rkt.t[:], RSTDKV[:, tsl], writes=[rkt.b])
                    dma("sp", Ct.t[:], ROPEC[:, tsl], writes=[Ct.b]); dma("sp", St.t[:], ROPES[:, tsl], writes=[St.b])
                    for h in range(8):
                        p = psr.get()
                        for kc in range(4):
                            mm(p, p.t[:, :], wqn.t[:, kc, h, :], cqt.t[:, kc, :], kc == 0, kc == 3, [wqn.b, cqt.b])
                        s_ = stg.get()
                        op("dve", lambda e: e.tensor_tensor(out=s_.t[:], in0=p.t[:], in1=rqt.t[:], op=ALU.mult), reads=[p.b, rqt.b], writes=[s_.b])
                        dma("sp", QNT[h * 128:(h + 1) * 128, tsl], s_.t[:], reads=[s_.b])
                        p = psr.get()
                        for kc in range(2):
                            mm(p, p.t[:, :], wkv.t[:, kc, h, 0:128], ckt.t[:, kc, :], kc == 0, kc == 1, [wkv.b, ckt.b])
                        s_ = stg.get()
                        op("dve", lambda e: e.tensor_tensor(out=s_.t[:], in0=p.t[:], in1=rkt.t[:], op=ALU.mult), reads=[p.b, rkt.b], writes=[s_.b])
                        dma("sp", KNT[h * 128:(h + 1) * 128, tsl], s_.t[:], reads=[s_.b])
                    for hp in range(4):
                        p = psr.get()
                        for kc in range(4):
                            mm(p, p.t[:, :], wqr.t[:, kc, 2 * hp:2 * hp + 2, :].rearrange("p a b -> p (a b)"), cqt.t[:, kc, :], kc == 0, kc == 3, [wqr.b, cqt.b])
                        rope_epilogue(p, 128, Ct, St, slice(0, 512), QRT[hp * 128:(hp + 1) * 128, tsl], tp, pre_scale=rqt)
                    for t4 in range(4):
                        tile = ts * 4 + t4
                        for half in range(2):
                            p = psr.get()
                            for kc in range(2):
                                mm(p, p.t[:, :].rearrange("p (h e) -> p h e", h=4), ckt.t[:, kc, t4 * 128:(t4 + 1) * 128], wkv.t[:, kc, half * 4:(half + 1) * 4, 128:256], kc == 0, kc == 1, [wkv.b, ckt.b])
                            s_ = stg.get()
                            op("act", lambda e: e.activation(out=s_.t[:], in_=p.t[:], func=AF.Copy, scale=rstdkvt.t[:, tile:tile + 1]), reads=[p.b, rstdkvt.b], writes=[s_.b])
                            dma("sp", VC[tile * 128:(tile + 1) * 128, half * 512:(half + 1) * 512], s_.t[:], reads=[s_.b])
            with cx.phase():
                qn_ = Rot(cx.sbn(2, [128, S], BF16, "qn")); qr_ = Rot(cx.sbn(2, [64, S], BF16, "qr")); kn_ = Rot(cx.sbn(2, [128, S], BF16, "kn"))
                krt = cx.sb([64, S], BF16, "kr"); vr_ = Rot(cx.sbn(2, [128, NT, 128], BF16, "V"))
                dma("sp", krt.t[:], KRT[:, :], writes=[krt.b])
                Ep = Rot(cx.sbn(4, [128, 512], BF16, "E"))
                psS = Rot(PS[0:3]); psO = [PS[3], PS[4]]; psN = [PS[5], PS[6]]
                rec = Rot(cx.sbn(2, [128, 512], F32, "rec")); obf = Rot(cx.sbn(2, [128, 512], BF16, "obf"))
                sc = float(192 ** -0.5)
                k_ = 0
                for h in range(8):
                    qn = qn_.get(); qr = qr_.get(); kn = kn_.get(); V = vr_.get()
                    dma("sp", qn.t[:], QNT[h * 128:(h + 1) * 128, :], writes=[qn.b])
                    dma("sp", qr.t[:], QRT[h * 64:(h + 1) * 64, :], writes=[qr.b])
                    dma("sp", kn.t[:], KNT[h * 128:(h + 1) * 128, :], writes=[kn.b])
                    dma("sp", V.t[:], VC[:, h * 128:(h + 1) * 128].rearrange("(kt p) e -> p kt e", p=128), writes=[V.b])
                    for qb in range(8):
                        maps = [[(kn, slice(0, 128), qn, slice(0, 128)), (krt, slice(0, 64), qr, slice(0, 64))]]
                        psOk = [psO[k_ % 2]]; psNk = [psN[k_ % 2]]; k_ += 1
                        (o, n), = attn_core(maps, V, qb, psS, psOk, psNk, Ep, sc)
                        r_ = rec.get(); ob_ = obf.get()
                        op("dve", lambda e: e.reciprocal(out=r_.t[:], in_=n.t[:]), reads=[n.b], writes=[r_.b])
                        op("dve", lambda e: e.tensor_tensor(out=ob_.t[:], in0=o.t[:], in1=r_.t[:], op=ALU.mult), reads=[o.b, r_.b], writes=[ob_.b])
                        dma("sp", OCT[h * 128:(h + 1) * 128, qb * 512:(qb + 1) * 512], ob_.t[:], reads=[ob_.b])
            if "oc" in dbg and l == 0:
                DBG.update(OCT=OCT, QNT=QNT, QRT=QRT, KNT=KNT, VC=VC)
            if "stopP4" in dbg:
                break

            with cx.phase():
              rnorm = cx.sb([128, 2048], F32, "rnorm")
              with cx.phase():
                colp = cx.sb([128, 64], F32, "colp")
                dma("sp", colp.t[:], A["colp"][l], writes=[colp.b])
                zT = cx.sb([33, S], F32, "zT"); dma("sp", zT.t[:], A["zT"][:, :], writes=[zT.b])
                w1 = cx.sb([33, 64], F32, "w1"); w2 = cx.sb([64, 64], F32, "w2"); w3 = cx.sb([64, 64], F32, "w3"); wo = cx.sb([64, 4096], F32, "wo")
                dma("sp", w1.t[:], A["hy_f_w1"][l], writes=[w1.b]); dma("sp", w2.t[:], A["hy_f_w2"][l], writes=[w2.b])
                dma("sp", w3.t[:], A["hy_f_w3"][l], writes=[w3.b]); dma("sp", wo.t[:], A["hy_f_wout"][l], writes=[wo.b])
                fb = cx.sb([64, 4], F32, "fb")
                for i in range(3):
                    op("dve", lambda e, i=i: e.tensor_tensor(out=fb.t[:, i:i + 1], in0=colp.t[0:64, 55 + i:56 + i], in1=colp.t[0:64, 58:59], op=ALU.mult), reads=[colp.b], writes=[fb.b])
                hA = cx.sb([64, S], F32, "hA"); hB = cx.sb([64, S], F32, "hB")
                ti5 = cx.sb([64, 512], I32, "ti5"); tf5 = cx.sb([64, 512], F32, "tf5")
                psr = Rot(PS[4:7])
                src, K = zT, 33
                for i, (wi, dstt) in enumerate(((w1, hA), (w2, hB), (w3, hA))):
                    for tb in range(8):
                        tsl = slice(tb * 512, (tb + 1) * 512)
                        p = psr.get()
                        mm(p, p.t[0:64, :], wi.t[0:K, :], src.t[0:K, tsl], True, True, [wi.b, src.b])
                        op("dve", lambda e: e.tensor_scalar(out=dstt.t[:, tsl], in0=p.t[0:64, :], scalar1=colp.t[0:64, 58:59], scalar2=fb.t[:, i:i + 1], op0=ALU.mult, op1=ALU.add), reads=[p.b, colp.b, fb.b], writes=[dstt.b])
                        sin_inplace(dstt, dstt.t[:, tsl], ti5, tf5)
                    src, K = dstt, 64
                h3 = hA
                dec = Rot(cx.sbn(2, [128, 1024], F32, "dec")); hf = Rot(cx.sbn(2, [128, 4096], F32, "hf"))
                ab = Rot(cx.sbn(2, [128, 4096], BF16, "ab")); ab2 = Rot(cx.sbn(2, [128, 2048], BF16, "ab2"))
                spt = Rot(cx.sbn(2, [128, 2048], BF16, "spt")); smt = Rot(cx.sbn(2, [128, 2048], BF16, "smt"))
                for tt in range(NT):
                    dc = dec.get(); hft = hf.get()
                    dma("sp", dc.t[:], A["decay"][tt * 128:(tt + 1) * 128, :], writes=[dc.b])
                    for cbk in range(8):
                        p = psr.get()
                        mm(p, p.t[:, :], h3.t[:, tt * 128:(tt + 1) * 128], wo.t[:, cbk * 512:(cbk + 1) * 512], True, True, [h3.b, wo.b])
                        ch0 = (cbk % 2) * 512
                        op("dve", lambda e: e.tensor_tensor(out=hft.t[:, cbk * 512:(cbk + 1) * 512], in0=p.t[:], in1=dc.t[:, ch0:ch0 + 512], op=ALU.mult), reads=[p.b, dc.b], writes=[hft.b])
                    hv = hft.t[:].rearrange("p (o d c) -> p o d c", o=2, d=2)
                    if tt == 0:
                        op("dve", lambda e: e.tensor_tensor(out=hv[0:1, :, 0, :], in0=hv[0:1, :, 0, :], in1=hv[0:1, :, 1, :], op=ALU.add), reads=[hft.b], writes=[hft.b])
                        op("dve", lambda e: e.memset(hv[0:1, :, 1, :], 0.0), reads=[hft.b], writes=[hft.b])
                    a_ = ab.get(); a2_ = ab2.get(); sp_ = spt.get(); sm_ = smt.get()
                    op("act", lambda e: e.activation(out=a_.t[:], in_=hft.t[:], func=AF.Abs), reads=[hft.b], writes=[a_.b])
                    av = a_.t[:].rearrange("p (o d c) -> p o d c", o=2, d=2)
                    op("pool", lambda e: e.tensor_tensor(out=a2_.t[:].rearrange("p (o c) -> p o c", o=2), in0=av[:, :, 0, :], in1=av[:, :, 1, :], op=ALU.add), reads=[a_.b], writes=[a2_.b])
                    for cbk in range(4):
                        mm(PS[cbk], PS[cbk].t[:, :], ones.t[:], a2_.t[:, cbk * 512:(cbk + 1) * 512], tt == 0, tt == NT - 1, [ones.b, a2_.b])
                    op("dve", lambda e: e.tensor_tensor(out=sp_.t[:].rearrange("p (o c) -> p o c", o=2), in0=hv[:, :, 0, :], in1=hv[:, :, 1, :], op=ALU.add), reads=[hft.b], writes=[sp_.b])
                    op("pool", lambda e: e.tensor_tensor(out=sm_.t[:].rearrange("p (o c) -> p o c", o=2), in0=hv[:, :, 0, :], in1=hv[:, :, 1, :], op=ALU.subtract), reads=[hft.b], writes=[sm_.b])
                    dma("sp", SPB[tt * 128:(tt + 1) * 128, :], sp_.t[:], reads=[sp_.b])
                    dma("sp", SMB[tt * 128:(tt + 1) * 128, :], sm_.t[:], reads=[sm_.b])
                for cbk in range(4):
                    op("dve", lambda e, cbk=cbk: e.reciprocal(out=rnorm.t[:, cbk * 512:(cbk + 1) * 512], in_=PS[cbk].t[:]), reads=[PS[cbk].b], writes=[rnorm.b])
              with cx.phase():
                zs = cx.sb([128, NT, 512], BF16, "zs"); zm = cx.sb([128, NT, 512], BF16, "zm")
                wf = Rot(cx.sbn(3, [128, NT, 128], BF16, "wf")); hs = Rot(cx.sbn(3, [128, 512], F32, "hs"))
                ps2 = Rot(PS[0:4])
                for cbk in range(4):
                    csl = slice(cbk * 512, (cbk + 1) * 512)
                    dma("sp", zs.t[:], SPB[:, csl].rearrange("(tc p) c -> p tc c", p=128), writes=[zs.b])
                    dma("sp", zm.t[:], SMB[:, csl].rearrange("(tc p) c -> p tc c", p=128), writes=[zm.b])
                    for kt in range(64):
                        w = wf.get(); zz = zs if kt < 32 else zm
                        dma("sp", w.t[:], A["wft"][kt], writes=[w.b])
                        p = ps2.get()
                        for tc in range(NT):
                            mm(p, p.t[:, :], w.t[:, tc, :], zz.t[:, tc, :], tc == 0, tc == NT - 1, [w.b, zz.b])
                        h_ = hs.get()
                        op("dve", lambda e: e.tensor_tensor(out=h_.t[:], in0=p.t[:], in1=rnorm.t[:, csl], op=ALU.mult), reads=[p.b, rnorm.b], writes=[h_.b])
                        dma("sp", HSPEC[kt * 128:(kt + 1) * 128, csl], h_.t[:], reads=[h_.b])

            if "stopP5" in dbg:
                break
            with cx.phase():
                Wc = [load_row_bc(A["hy_conv_w"][l, j:j + 1, :], 3072, "wc%d" % j) for j in range(3)]
                Bc = load_row_bc(A["hy_conv_b"][l], 3072, "bc")
                ur = [Rot(cx.sbn(2, [128, 1024], F32, "u%d" % j)) for j in range(3)]
                acc = Rot(cx.sbn(2, [128, 1024], F32, "acc")); tq = Rot(cx.sbn(2, [128, 1024], F32, "tq")); vb = Rot(cx.sbn(2, [128, 1024], BF16, "vb"))
                for tt in range(NT):
                    for part, dstf in enumerate((VF, X1F, X2F)):
                        csl = slice(part * 1024, (part + 1) * 1024)
                        U = []
                        for j in range(3):
                            u = ur[j].get()
                            dma("sp", u.t[:], USC[tt * 128 + j: tt * 128 + j + 128, csl], writes=[u.b])
                            U.append(u)
                        a_ = acc.get(); t_ = tq.get()
                        op("dve", lambda e: e.tensor_tensor(out=a_.t[:], in0=U[0].t[:], in1=Wc[0].t[:, csl], op=ALU.mult), reads=[U[0].b, Wc[0].b], writes=[a_.b])
                        op("pool", lambda e: e.tensor_tensor(out=t_.t[:], in0=U[1].t[:], in1=Wc[1].t[:, csl], op=ALU.mult), reads=[U[1].b, Wc[1].b], writes=[t_.b])
                        op("dve", lambda e: e.tensor_tensor(out=a_.t[:], in0=a_.t[:], in1=t_.t[:], op=ALU.add), reads=[a_.b, t_.b], writes=[a_.b])
                        op("pool", lambda e: e.tensor_tensor(out=t_.t[:], in0=U[2].t[:], in1=Wc[2].t[:, csl], op=ALU.mult), reads=[U[2].b, Wc[2].b], writes=[t_.b])
                        op("dve", lambda e: e.tensor_tensor(out=a_.t[:], in0=a_.t[:], in1=t_.t[:], op=ALU.add), reads=[a_.b, t_.b], writes=[a_.b])
                        op("dve", lambda e: e.tensor_tensor(out=a_.t[:], in0=a_.t[:], in1=Bc.t[:, csl], op=ALU.add), reads=[a_.b, Bc.b], writes=[a_.b])
                        dma("sp", dstf[tt * 128:(tt + 1) * 128, :], a_.t[:], reads=[a_.b])
                        if part == 0:
                            v_ = vb.get()
                            op("act", lambda e: e.copy(out=v_.t[:], in_=a_.t[:]), reads=[a_.b], writes=[v_.b])
                            dma("sp", VB[tt * 128:(tt + 1) * 128, :], v_.t[:], reads=[v_.b])

            if "stopP6" in dbg:
                break
            for o_ in range(2):
                ZB_in, ZF_in, GATE = (VB, VF, X1F) if o_ == 0 else (Z1B, Z1F, X2F)
                with cx.phase():
                    skp = load_row_bc(A["hy_skip"][l, o_:o_ + 1, :], 1024, "skip")
                    z = cx.sb([128, NT, 512], BF16, "z"); Y = cx.sb([128, 64, 512], BF16, "Y")
                    wp = Rot(cx.sbn(2, [128, 64, 128], BF16, "wp"))
                    hr_ = Rot(cx.sbn(2, [128, 512], F32, "hr")); hi_ = Rot(cx.sbn(2, [128, 512], F32, "hi"))
                    m_ = [Rot(cx.sbn(1, [128, 512], F32, "m%d" % i)) for i in range(4)]
                    zf_ = Rot(cx.sbn(2, [128, 512], F32, "zf")); gt_ = Rot(cx.sbn(2, [128, 512], F32, "gt"))
                    y2_ = Rot(cx.sbn(2, [128, 512], F32, "y2")); zo_ = Rot(cx.sbn(2, [128, 512], F32, "zo")); zb_ = Rot(cx.sbn(2, [128, 512], BF16, "zb"))
                    ot_ = Rot(cx.sbn(2, [128, 4, 128], BF16, "ot"))
                    psX = Rot(PS[0:4]); psY = Rot(PS[4:7])
                    wslots = []
                    for w_ in wp.l:
                        for hf_ in range(2):
                            tl_ = Tl(w_.t); tl_.lo = hf_ * 32
                            wslots.append(tl_)
                    wsl = Rot(wslots)
                    tcount = 0
                    for cb in range(2):
                        csl = slice(cb * 512, (cb + 1) * 512)
                        hsl = slice(o_ * 1024 + cb * 512, o_ * 1024 + (cb + 1) * 512)
                        dma("sp", z.t[:], ZB_in[:, csl].rearrange("(tc p) c -> p tc c", p=128), writes=[z.b])
                        for j in range(32):
                            X = []
                            for kt in (j, 32 + j):
                                w = wsl.get()
                                dma("sp", w.t[:, w.lo:w.lo + 32, :], A["wft"][kt], writes=[w.b])
                                p = psX.get()
                                for tc in range(NT):
                                    mm(p, p.t[:, :], w.t[:, w.lo + tc, :], z.t[:, tc, :], tc == 0, tc == NT - 1, [w.b, z.b])
                                X.append(p)
                            hr = hr_.get(); hi = hi_.get()
                            dma("sp", hr.t[:], HSPEC[j * 128:(j + 1) * 128, hsl], writes=[hr.b])
                            dma("sp", hi.t[:], HSPEC[4096 + j * 128:4096 + (j + 1) * 128, hsl], writes=[hi.b])
                            m = [r.get() for r in m_]
                            op("dve", lambda e: e.tensor_tensor(out=m[0].t[:], in0=X[0].t[:], in1=hr.t[:], op=ALU.mult), reads=[X[0].b, hr.b], writes=[m[0].b])
                            op("dve", lambda e: e.tensor_tensor(out=m[1].t[:], in0=X[1].t[:], in1=hi.t[:], op=ALU.mult), reads=[X[1].b, hi.b], writes=[m[1].b])
                            op("dve", lambda e: e.tensor_tensor(out=m[2].t[:], in0=X[0].t[:], in1=hi.t[:], op=ALU.mult), reads=[X[0].b, hi.b], writes=[m[2].b])
                            op("dve", lambda e: e.tensor_tensor(out=m[3].t[:], in0=X[1].t[:], in1=hr.t[:], op=ALU.mult), reads=[X[1].b, hr.b], writes=[m[3].b])
                            op("pool", lambda e: e.tensor_tensor(out=Y.t[:, j, :], in0=m[0].t[:], in1=m[1].t[:], op=ALU.subtract), reads=[m[0].b, m[1].b], writes=[Y.b])
                            op("pool", lambda e: e.tensor_tensor(out=Y.t[:, 32 + j, :], in0=m[2].t[:], in1=m[3].t[:], op=ALU.add), reads=[m[2].b, m[3].b], writes=[Y.b])
                        cx.barrier()
                        for tt in range(NT):
                            w = wp.get()
                            dma("sp", w.t[:], A["wit"][tt], writes=[w.b])
                            p = psY.get()
                            for kc in range(64):
                                mm(p, p.t[:, :], w.t[:, kc, :], Y.t[:, kc, :], kc == 0, kc == 63, [w.b, Y.b])
                            zf = zf_.get(); gt = gt_.get(); y2 = y2_.get(); zo = zo_.get(); zb = zb_.get()
                            rsl = slice(tt * 128, (tt + 1) * 128)
                            dma("sp", zf.t[:], ZF_in[rsl, csl], writes=[zf.b])
                            dma("sp", gt.t[:], GATE[rsl, csl], writes=[gt.b])
                            op("pool", lambda e: e.tensor_tensor(out=zf.t[:], in0=zf.t[:], in1=skp.t[:, csl], op=ALU.mult), reads=[zf.b, skp.b], writes=[zf.b])
                            op("dve", lambda e: e.tensor_tensor(out=y2.t[:], in0=p.t[:], in1=zf.t[:], op=ALU.add), reads=[p.b, zf.b], writes=[y2.b])
                            if o_ == 0:
                                op("dve", lambda e: e.tensor_tensor(out=zo.t[:], in0=y2.t[:], in1=gt.t[:], op=ALU.mult), reads=[y2.b, gt.b], writes=[zo.b])
                                dma("sp", Z1F[rsl, csl], zo.t[:], reads=[zo.b])
                                op("act", lambda e: e.copy(out=zb.t[:], in_=zo.t[:]), reads=[zo.b], writes=[zb.b])
                                dma("sp", Z1B[rsl, csl], zb.t[:], reads=[zb.b])
                            else:
                                op("dve", lambda e: e.tensor_tensor(out=zo.t[:], in0=y2.t[:], in1=gt.t[:], op=ALU.mult), reads=[y2.b, gt.b], writes=[zo.b])
                                hh = PS[7]
                                for q4 in range(4):
                                    op("pe", lambda e, q4=q4: e.transpose(hh.t[:, q4 * 128:(q4 + 1) * 128], zo.t[:, q4 * 128:(q4 + 1) * 128], identf.t[:]), reads=[zo.b, identf.b], writes=[hh.b])
                                ot = ot_.get()
                                op("act", lambda e: e.copy(out=ot.t[:], in_=hh.t[:, :].rearrange("p (a b) -> p a b", a=4)), reads=[hh.b], writes=[ot.b])
                                dma("sp", OBT[cb * 512:(cb + 1) * 512, rsl].rearrange("(a p) t -> p a t", p=128), ot.t[:], reads=[ot.b])
                        cx.barrier()
            if "ob" in dbg and l == 0:
                DBG.update(OBT=OBT, HSPEC=HSPEC, VF=VF, X1F=X1F, Z1F=Z1F)
            if "stopP8" in dbg:
                break

            with cx.phase():
                WB = []
                for nm in ("w_branch_a", "w_branch_b", "w_branch_c"):
                    w = cx.sb([128, 8, D], BF16, nm)
                    for q in range(4):
                        dma("pool", w.t[:, :, q * 512:(q + 1) * 512], A[nm][l][:, q * 512:(q + 1) * 512].rearrange("(kc p) c -> p kc c", p=128), writes=[w.b])
                    WB.append(w)
                ob_ = [Rot(cx.sbn(2, [128, 8, 512], BF16, "o%d" % i)) for i in range(3)]
                gr = Rot(cx.sbn(2, [128, 3, 512], BF16, "g"))
                mA = Rot(cx.sbn(2, [128, 512], F32, "mA")); mB = Rot(cx.sbn(2, [128, 512], F32, "mB")); mC = Rot(cx.sbn(2, [128, 512], F32, "mC")); mo = Rot(cx.sbn(2, [128, 512], BF16, "mo"))
                psr = Rot(PS[0:6])
                GTv = GT.rearrange("(b r) t -> r b t", b=3)
                for ts in range(8):
                    tsl = slice(ts * 512, (ts + 1) * 512)
                    ob3 = []
                    for i, src in enumerate((OAT, OBT, OCT)):
                        t_ = ob_[i].get()
                        dma("sp", t_.t[:], src.rearrange("(kc p) t -> p kc t", p=128)[:, :, tsl], writes=[t_.b])
                        ob3.append(t_)
                    for dmc in range(16):
                        g_ = gr.get()
                        dma("sp", g_.t[:], GTv[dmc * 128:(dmc + 1) * 128, :, tsl], writes=[g_.b])
                        pp = []
                        for i in range(3):
                            p = psr.get()
                            for kc in range(8):
                                mm(p, p.t[:, :], WB[i].t[:, kc, dmc * 128:(dmc + 1) * 128], ob3[i].t[:, kc, :], kc == 0, kc == 7, [WB[i].b, ob3[i].b])
                            pp.append(p)
                        a_, b_, c_, o__ = mA.get(), mB.get(), mC.get(), mo.get()
                        op("dve", lambda e: e.tensor_tensor(out=a_.t[:], in0=pp[0].t[:], in1=g_.t[:, 0, :], op=ALU.mult), reads=[pp[0].b, g_.b], writes=[a_.b])
                        op("dve", lambda e: e.tensor_tensor(out=b_.t[:], in0=pp[1].t[:], in1=g_.t[:, 1, :], op=ALU.mult), reads=[pp[1].b, g_.b], writes=[b_.b])
                        op("dve", lambda e: e.tensor_tensor(out=c_.t[:], in0=pp[2].t[:], in1=g_.t[:, 2, :], op=ALU.mult), reads=[pp[2].b, g_.b], writes=[c_.b])
                        op("pool", lambda e: e.tensor_tensor(out=a_.t[:], in0=a_.t[:], in1=b_.t[:], op=ALU.add), reads=[a_.b, b_.b], writes=[a_.b])
                        op("pool", lambda e: e.tensor_tensor(out=o__.t[:], in0=a_.t[:], in1=c_.t[:], op=ALU.add), reads=[a_.b, c_.b], writes=[o__.b])
                        dma("sp", MT[dmc * 128:(dmc + 1) * 128, tsl], o__.t[:], reads=[o__.b])
            if "merged" in dbg and l == 0:
                DBG.update(MT=MT)
            if "stopP9" in dbg:
                break

            with cx.phase():
                Wo = cx.sbn(4, [128, 16, 512], BF16, "wo")
                for nb in range(4):
                    dma("pool", Wo[nb].t[:], A["w_out"][l][:, nb * 512:(nb + 1) * 512].rearrange("(kc p) c -> p kc c", p=128), writes=[Wo[nb].b])
                Gt = load_row_bc(A["ln1_g"][l], D, "G"); Bt = load_row_bc(A["ln1_b"][l], D, "B")
                tmp = ln_tmp(); xr = Rot(cx.sbn(2, [128, D], F32, "x")); mr = Rot(cx.sbn(2, [128, 16, 512], BF16, "mT"))
                psr = Rot(PS[0:6])
                MTv = MT.rearrange("(kc p) t -> p kc t", p=128)
                for tb in range(8):
                    mT = mr.get()
                    dma("sp", mT.t[:], MTv[:, :, tb * 512:(tb + 1) * 512], writes=[mT.b])
                    for t4 in range(4):
                        tok0 = tb * 512 + t4 * 128
                        xt = xr.get()
                        dma("sp", xt.t[:], H32[tok0:tok0 + 128, :], writes=[xt.b])
                        for nb in range(4):
                            p = psr.get()
                            for kc in range(16):
                                mm(p, p.t[:, :], mT.t[:, kc, t4 * 128:(t4 + 1) * 128], Wo[nb].t[:, kc, :], kc == 0, kc == 15, [mT.b, Wo[nb].b])
                            op("dve", lambda e, nb=nb: e.scalar_tensor_tensor(out=xt.t[:, nb * 512:(nb + 1) * 512], in0=xt.t[:, nb * 512:(nb + 1) * 512], scalar=ALPHA, in1=p.t[:], op0=ALU.mult, op1=ALU.add), reads=[xt.b, p.b], writes=[xt.b])
                        ln_tail(xt, Gt, Bt, tok0, H32, tmp)
            if "h1" in dbg and l == 0:
                DBG.update(H1=H32)
                break

            with cx.phase():
                Gt = load_row_bc(A["ln2_g"][l], D, "G"); Bt = load_row_bc(A["ln2_b"][l], D, "B")
                tmp = ln_tmp()
                wr = Rot(cx.sbn(3, [128, 16, 512], BF16, "w"))
                hb = Rot(cx.sbn(2, [128, 16, 512], BF16, "hTb"))
                aT = cx.sb([128, 32, 512], BF16, "aT")
                xs = cx.sbn(4, [128, D], F32, "x")
                rl = Rot(cx.sbn(2, [128, 512], F32, "rl"))
                psr = Rot(PS[0:2]); psa = PS[2:6]
                HTv = HT.rearrange("(kc p) t -> p kc t", p=128)
                W1 = A["mlp_w1"][l]; W2 = A["mlp_w2"][l]
                dst = OUT if l == NL - 1 else H32
                for tb in range(8):
                    hT = hb.get()
                    dma("sp", hT.t[:], HTv[:, :, tb * 512:(tb + 1) * 512], writes=[hT.b])
                    for t4 in range(4):
                        tok0 = tb * 512 + t4 * 128
                        dma("sp", xs[t4].t[:], H32[tok0:tok0 + 128, :], writes=[xs[t4].b])
                    for half in range(2):
                        for pc in range(8):
                            w = wr.get()
                            c0 = (half * 8 + pc) * 512
                            dma("pool", w.t[:], W1[:, c0:c0 + 512].rearrange("(kc p) c -> p kc c", p=128), writes=[w.b])
                            for ct in range(4):
                                p = psr.get()
                                for kc in range(16):
                                    mm(p, p.t[:, :], w.t[:, kc, ct * 128:(ct + 1) * 128], hT.t[:, kc, :], kc == 0, kc == 15, [w.b, hT.b])
                                r_ = rl.get()
                                op("act", lambda e: e.activation(out=r_.t[:], in_=p.t[:], func=AF.Relu), reads=[p.b], writes=[r_.b])
                                op("dve", lambda e: e.tensor_tensor(out=aT.t[:, pc * 4 + ct, :], in0=r_.t[:], in1=r_.t[:], op=ALU.mult), reads=[r_.b], writes=[aT.b])
                        for nb in range(4):
                            for kg in range(2):
                                w = wr.get()
                                r0 = (half * 32 + kg * 16) * 128
                                dma("pool", w.t[:], W2[r0:r0 + 2048, nb * 512:(nb + 1) * 512].rearrange("(kc p) c -> p kc c", p=128), writes=[w.b])
                                for t4 in range(4):
                                    p = psa[t4]
                                    for kc in range(16):
                                        mm(p, p.t[:, :], aT.t[:, kg * 16 + kc, t4 * 128:(t4 + 1) * 128], w.t[:, kc, :], kg == 0 and kc == 0, kg == 1 and kc == 15, [aT.b, w.b])
                            for t4 in range(4):
                                p = psa[t4]; xt = xs[t4]
                                if half == 0:
                                    op("dve", lambda e, nb=nb: e.scalar_tensor_tensor(out=xt.t[:, nb * 512:(nb + 1) * 512], in0=xt.t[:, nb * 512:(nb + 1) * 512], scalar=ALPHA, in1=p.t[:], op0=ALU.mult, op1=ALU.add), reads=[xt.b, p.b], writes=[xt.b])
                                else:
                                    op("dve", lambda e, nb=nb: e.tensor_tensor(out=xt.t[:, nb * 512:(nb + 1) * 512], in0=xt.t[:, nb * 512:(nb + 1) * 512], in1=p.t[:], op=ALU.add), reads=[xt.b, p.b], writes=[xt.b])
                    for t4 in range(4):
                        ln_tail(xs[t4], Gt, Bt, tb * 512 + t4 * 128, dst, tmp)

        if "h2" in dbg:
            DBG.update(H2=H32)
        if dbg:
            for k, apv in DBG.items():
                o = nc.dram_tensor("dbg_" + k, list(apv.shape), apv.dtype, kind="ExternalOutput").ap()
                n = apv.shape[0]
                step = max(1, (n + 7) // 8)
                for r in range(0, n, step):
                    e_ = min(n, r + step)
                    dma("sp", o[r:e_, :], apv[r:e_, :])
        cx.barrier()
    return nc


_CONST = {}


def _consts():
    if _CONST:
        return _CONST
    N = 8192
    n = np.arange(S, dtype=np.float64)[:, None]
    k = np.arange(S, dtype=np.float64)[None, :]
    th = 2.0 * np.pi * (k + 0.5) / N
    Cm = np.cos(th * n); Sm = np.sin(th * n)
    WF = np.concatenate([Cm, Sm], axis=1)
    wft = WF.reshape(32, 128, 64, 128).transpose(2, 1, 0, 3)
    WI = (2.0 / N) * WF.T
    wit = WI.reshape(64, 128, 32, 128).transpose(2, 1, 0, 3)
    _CONST["wft"] = np.ascontiguousarray(wft).astype(ml_dtypes.bfloat16)
    _CONST["wit"] = np.ascontiguousarray(wit).astype(ml_dtypes.bfloat16)
    t = np.linspace(0.0, 1.0, S, dtype=np.float32)[:, None]
    bands = 16
    w = (np.float32(2.0 * math.pi) * np.arange(S, dtype=np.float32)[:, None] / np.float32(S)).astype(np.float32)
    f = np.linspace(1e-4, bands - 1, bands, dtype=np.float32)[None, :]
    angz = (f * w).astype(np.float32)
    z = np.concatenate([t, np.cos(angz), -np.sin(angz)], axis=-1).astype(np.float32)
    deltas = np.linspace(math.log(1e-2) / 0.3, math.log(1e-2) / 1.5, 1024, dtype=np.float32)
    decay = np.exp(-t * np.abs(deltas)[None, :]).astype(np.float32)
    _CONST["zT"] = np.ascontiguousarray(z.T)
    _CONST["decay"] = np.ascontiguousarray(decay)
    cst = np.zeros((128, 8), np.float32)
    inv = (10000.0 ** (-np.arange(0, 64, 2, dtype=np.float32) / np.float32(64))).astype(np.float32)
    p = np.arange(128)
    cst[:, 0] = inv[p % 32]
    cst[:, 1] = np.where((p % 64) < 32, -1.0, 1.0)
    cst[:, 2] = -math.pi
    cst[:, 3] = LN_EPS
    cst[:, 4] = RMS_EPS
    _CONST["cst"] = cst
    cm = np.zeros((128, 256), np.float32)
    cm[p, p] = 1.0
    sw = (p // 64) * 64 + ((p % 64) + 32) % 64
    cm[sw, 128 + p] = 1.0
    _CONST["cmat"] = cm.astype(ml_dtypes.bfloat16)
    _CONST["identf"] = np.eye(128, dtype=np.float32)
    return _CONST


def make_in_maps(inputs, cores, names=None):
    f = lambda a: np.ascontiguousarray(np.asarray(a, dtype=np.float32))
    C = _consts()
    L = NL
    colp = np.zeros((L, 128, 64), np.float32)
    colp[:, :, 0:48] = f(inputs["gate_b"]).reshape(L, 48, 128).transpose(0, 2, 1)
    colp[:, :, 48:52] = f(inputs["mla_q_norm_g"]).reshape(L, 4, 128).transpose(0, 2, 1)
    colp[:, :, 52:54] = f(inputs["mla_kv_norm_g"]).reshape(L, 2, 128).transpose(0, 2, 1)
    colp[:, :, 54] = f(inputs["da_subln_g"])
    colp[:, 0:64, 55] = f(inputs["hy_f_b1"]); colp[:, 0:64, 56] = f(inputs["hy_f_b2"]); colp[:, 0:64, 57] = f(inputs["hy_f_b3"])
    colp[:, 0:64, 58] = f(inputs["hy_f_freq"])
    shared = {
        "ln_emb_g": f(inputs["ln_emb_g"]).reshape(1, D), "ln_emb_b": f(inputs["ln_emb_b"]).reshape(1, D),
        "w_in": f(inputs["w_in"]), "da_lambda": f(inputs["da_lambda"]).reshape(L, 1, 256),
        "hy_conv_w": f(inputs["hy_conv_w"]), "hy_conv_b": f(inputs["hy_conv_b"]).reshape(L, 1, 3072),
        "hy_f_w1": f(inputs["hy_f_w1"]), "hy_f_w2": f(inputs["hy_f_w2"]), "hy_f_w3": f(inputs["hy_f_w3"]),
        "hy_f_wout": f(inputs["hy_f_wout"]), "hy_skip": f(inputs["hy_skip"]),
        "mla_w_uq": f(inputs["mla_w_uq"]), "mla_w_ukv": f(inputs["mla_w_ukv"]),
        "w_branch_a": f(inputs["w_branch_a"]), "w_branch_b": f(inputs["w_branch_b"]), "w_branch_c": f(inputs["w_branch_c"]),
        "w_out": f(inputs["w_out"]),
        "ln1_g": f(inputs["ln1_g"]).reshape(L, 1, D), "ln1_b": f(inputs["ln1_b"]).reshape(L, 1, D),
        "mlp_w1": f(inputs["mlp_w1"]), "mlp_w2": f(inputs["mlp_w2"]),
        "ln2_g": f(inputs["ln2_g"]).reshape(L, 1, D), "ln2_b": f(inputs["ln2_b"]).reshape(L, 1, D),
        "colp": colp, "cst": C["cst"], "cmat": C["cmat"], "identf": C["identf"], "wft": C["wft"], "wit": C["wit"], "zT": C["zT"], "decay": C["decay"],
    }
    x = f(inputs["x"]); pos = np.ascontiguousarray(np.asarray(inputs["positions"], dtype=np.int32))
    maps = []
    for c in cores:
        b = c // 2
        m = dict(shared)
        m["x"] = x[b]; m["pos"] = pos[b].reshape(1, S)
        if names is not None:
            m = {k: v for k, v in m.items() if k in names}
        maps.append(m)
    return maps


def kernel(**inputs):
    nc = build(NL)
    cores = list(range(8))
    maps = make_in_maps(inputs, cores, set(nc._used_inputs.keys()))
    res = run_bass_kernel_spmd(nc, maps, core_ids=cores)
    out = np.stack([np.asarray(res.results[2 * b]["out"], dtype=np.float32) for b in range(4)], axis=0)
    return out
```
